# Optimizing a Trainium2 kernel written in Bass

```python
import jax, jax.numpy as jnp
from jax import lax
import numpy as np

D_MODEL = 1024
BATCH = 2
SEQ = 8192
DEPTH = 4
DEC_BATCH = 32
DEC_SEQ = 4
PAST_LEN = 8192
PAGE_SIZE = 128

HEAD_DIM = 64
A_W = D_MODEL // 4
B_W = D_MODEL // 2
C_W = D_MODEL // 4
A_HEADS = A_W // HEAD_DIM
B_HEADS = B_W // HEAD_DIM
C_HEADS = C_W // HEAD_DIM
MIX_W = A_W + B_W + C_W
N_KV = 2
KV_W = N_KV * HEAD_DIM
IDX_HEADS = 8
IDX_DIM = 64
TOPK_MAX = 256
ROT_DIM = HEAD_DIM // 4
ROPE_THETA = 500000.0
CONV_W = 3
CHUNK = 128
D_FF = 2816
Q_BLOCK = 128
LN_EPS = 1e-5
ALPHA = (2 * DEPTH) ** 0.25
BETA = (8 * DEPTH) ** -0.25
IN_SIZES = (A_W, A_W, A_W, B_W, KV_W, KV_W, IDX_HEADS * IDX_DIM, IDX_DIM, IDX_HEADS, C_W, C_W)
N_IN = sum(IN_SIZES)

kernel_name = "hybrid_conv_dsa_gmlp_decoder_step"

F32 = jnp.float32


def _layer_norm(x, g, b):
    xf = x.astype(F32)
    mu = jnp.mean(xf, axis=-1, keepdims=True)
    xc = xf - mu
    var = jnp.mean(xc * xc, axis=-1, keepdims=True)
    return (xc * lax.rsqrt(var + LN_EPS) * g.astype(F32) + b.astype(F32)).astype(x.dtype)


def _rotary(x, pos):
    half = ROT_DIM // 2
    inv = ROPE_THETA ** (-jnp.arange(half, dtype=F32) / half)
    ang = pos.astype(F32)[:, None] * inv[None, :]
    cos = jnp.cos(ang)[:, None, :]
    sin = jnp.sin(ang)[:, None, :]
    xr = x[..., :ROT_DIM].astype(F32)
    x1, x2 = xr[..., :half], xr[..., half:]
    rot = jnp.concatenate([x1 * cos - x2 * sin, x2 * cos + x1 * sin], axis=-1)
    return jnp.concatenate([rot.astype(x.dtype), x[..., ROT_DIM:]], axis=-1)


def _causal_dwconv(z, prev, w):
    T = z.shape[1]
    zz = jnp.concatenate([prev.astype(z.dtype), z], axis=1)
    y = zz[:, 0:T] * w[0]
    for i in range(1, CONV_W):
        y = y + zz[:, i:i + T] * w[i]
    return y, zz[:, -(CONV_W - 1):]


def _project(x, pos, w):
    h = x @ w
    offs = np.cumsum(IN_SIZES)[:-1].tolist()
    gb, gc, ha, q, k, v, qi, ki, wi, u, vc = jnp.split(h, offs, axis=-1)
    Bx, T = x.shape[:2]
    q = _rotary(q.reshape(Bx, T, B_HEADS, HEAD_DIM), pos)
    k = _rotary(k.reshape(Bx, T, N_KV, HEAD_DIM), pos)
    v = v.reshape(Bx, T, N_KV, HEAD_DIM)
    qi = _rotary(qi.reshape(Bx, T, IDX_HEADS, IDX_DIM), pos)
    ki = _rotary(ki.reshape(Bx, T, 1, IDX_DIM), pos)[:, :, 0]
    wi = wi * (IDX_HEADS ** -0.5)
    return gb, gc, ha, q, k, v, qi, ki, wi, u, vc


def _short_conv(gb, gc, ha, prev, w):
    y, new_prev = _causal_dwconv(gc * ha, prev, w)
    return gb * y, new_prev


def _index_scores(qi, wi, ki):
    dots = jnp.einsum('bqhd,bsd->bqhs', qi.astype(F32), ki.astype(F32)) * (IDX_DIM ** -0.5)
    return jnp.einsum('bqhs,bqh->bqs', jax.nn.relu(dots), wi.astype(F32))


def _attend(q, kg, vg, valid):
    Bq, Q = q.shape[:2]
    qg = q.reshape(Bq, Q, N_KV, B_HEADS // N_KV, HEAD_DIM).astype(F32)
    s = jnp.einsum('bqgrd,bqkgd->bqgrk', qg, kg.astype(F32)) * (HEAD_DIM ** -0.5)
    s = jnp.where(valid[:, :, None, None, :], s, -jnp.inf)
    p = jax.nn.softmax(s, axis=-1)
    o = jnp.einsum('bqgrk,bqkgd->bqgrd', p, vg.astype(F32))
    return o.reshape(Bq, Q, B_W).astype(q.dtype)


def _gather_rows(a, idx):
    return jax.vmap(lambda aa, ii: aa[ii])(a, idx)


def _dsa_prompt(q, k, v, qi, wi, ki):
    Bp, S = q.shape[:2]
    ksel = min(TOPK_MAX, S // 4)
    nblk = S // Q_BLOCK
    spos = jnp.arange(S, dtype=jnp.int32)

    def to_blocks(a):
        return jnp.moveaxis(a.reshape(Bp, nblk, Q_BLOCK, *a.shape[2:]), 1, 0)

    def one_block(args):
        qb, qib, wib, t0 = args
        tpos = t0 + jnp.arange(Q_BLOCK, dtype=jnp.int32)
        sc = _index_scores(qib, wib, ki)
        sc = jnp.where(spos[None, None, :] <= tpos[None, :, None], sc, -jnp.inf)
        _, idx = lax.top_k(sc, ksel)
        valid = idx <= tpos[None, :, None]
        return _attend(qb, _gather_rows(k, idx), _gather_rows(v, idx), valid)

    t0s = jnp.arange(nblk, dtype=jnp.int32) * Q_BLOCK
    out = lax.map(one_block, (to_blocks(q), to_blocks(qi), to_blocks(wi), t0s))
    return jnp.moveaxis(out, 0, 1).reshape(Bp, S, B_W)


def _dsa_sample(q, k_new, v_new, qi, wi, ki_new, pool_k, pool_v, pool_ki, page_table):
    Bd, T = q.shape[:2]
    n_pages = page_table.shape[1]
    past = n_pages * PAGE_SIZE
    L = past + T
    ksel = min(TOPK_MAX, L // 4)
    ki_past = pool_ki[page_table].reshape(Bd, past, IDX_DIM)
    ki_all = jnp.concatenate([ki_past.astype(ki_new.dtype), ki_new], axis=1)
    tpos = past + jnp.arange(T, dtype=jnp.int32)
    sc = _index_scores(qi, wi, ki_all)
    sc = jnp.where(jnp.arange(L, dtype=jnp.int32)[None, None, :] <= tpos[None, :, None], sc, -jnp.inf)
    _, idx = lax.top_k(sc, ksel)
    valid = idx <= tpos[None, :, None]
    in_past = (idx < past)[..., None, None]
    page = jnp.minimum(idx // PAGE_SIZE, n_pages - 1)
    phys = jax.vmap(lambda pt, pg: pt[pg])(page_table, page)
    off = idx % PAGE_SIZE
    new_i = jnp.clip(idx - past, 0, T - 1)
    kg = jnp.where(in_past, pool_k[phys, off].astype(k_new.dtype), _gather_rows(k_new, new_i))
    vg = jnp.where(in_past, pool_v[phys, off].astype(v_new.dtype), _gather_rows(v_new, new_i))
    return _attend(q, kg, vg, valid)


def _chunk_mlp(u, vc, g, b, ws, bs):
    vn = _layer_norm(vc, g, b)
    Bx, T = u.shape[:2]
    n = min(T, CHUNK)
    mask = jnp.tril(jnp.ones((n, n), dtype=bool))
    wm = jnp.where(mask[None], ws[:, :n, :n], 0)
    vr = vn.reshape(Bx, T // n, n, C_HEADS, HEAD_DIM)
    mixed = jnp.einsum('hts,bcshd->bcthd', wm, vr) + jnp.transpose(bs[:, :n])[None, None, :, :, None]
    return u * mixed.reshape(Bx, T, C_W), vn


def _conv_ffn(x, prev, w_up, w_conv, w_down):
    h, new_prev = _causal_dwconv(x @ w_up, prev, w_conv)
    g, u = jnp.split(h, 2, axis=-1)
    return (jax.nn.silu(g) * u) @ w_down, new_prev


def setup_inputs(seed: int = 0) -> dict:
    key = jax.random.key(seed)
    ks = jax.random.split(key, 24)
    n_pages = PAST_LEN // PAGE_SIZE
    n_pool = (DEC_BATCH * n_pages * 5) // 4
    nrm = jax.random.normal
    page_table = jax.random.permutation(ks[0], n_pool)[:DEC_BATCH * n_pages].reshape(DEC_BATCH, n_pages).astype(jnp.int32)
    return {
        "x_prompt": nrm(ks[1], (BATCH, SEQ, D_MODEL), F32),
        "x_sample": nrm(ks[2], (DEC_BATCH, DEC_SEQ, D_MODEL), F32),
        "cache_k": nrm(ks[3], (DEPTH, n_pool, PAGE_SIZE, N_KV, HEAD_DIM), F32),
        "cache_v": nrm(ks[4], (DEPTH, n_pool, PAGE_SIZE, N_KV, HEAD_DIM), F32),
        "cache_kidx": nrm(ks[5], (DEPTH, n_pool, PAGE_SIZE, IDX_DIM), F32),
        "page_table": page_table,
        "state_conv": nrm(ks[6], (DEPTH, DEC_BATCH, CONV_W - 1, A_W), F32),
        "state_ffn_conv": nrm(ks[7], (DEPTH, DEC_BATCH, CONV_W - 1, 2 * D_FF), F32),
        "w_in": nrm(ks[8], (DEPTH, D_MODEL, N_IN), F32) * D_MODEL ** -0.5,
        "conv_a": nrm(ks[9], (DEPTH, CONV_W, A_W), F32) * 0.5,
        "ln_cv_g": 1.0 + 0.02 * nrm(ks[10], (DEPTH, C_W), F32),
        "ln_cv_b": 0.02 * nrm(ks[11], (DEPTH, C_W), F32),
        "w_sp": nrm(ks[12], (DEPTH, C_HEADS, CHUNK, CHUNK), F32) * CHUNK ** -0.5,
        "b_sp": 1.0 + 0.02 * nrm(ks[13], (DEPTH, C_HEADS, CHUNK), F32),
        "w_o": nrm(ks[14], (DEPTH, MIX_W, D_MODEL), F32) * (MIX_W ** -0.5) * BETA,
        "ln1_g": 1.0 + 0.02 * nrm(ks[15], (DEPTH, D_MODEL), F32),
        "ln1_b": 0.02 * nrm(ks[16], (DEPTH, D_MODEL), F32),
        "w_up": nrm(ks[17], (DEPTH, D_MODEL, 2 * D_FF), F32) * D_MODEL ** -0.5,
        "conv_f": nrm(ks[18], (DEPTH, CONV_W, 2 * D_FF), F32) * 0.5,
        "w_down": nrm(ks[19], (DEPTH, D_FF, D_MODEL), F32) * (D_FF ** -0.5) * BETA,
        "ln2_g": 1.0 + 0.02 * nrm(ks[20], (DEPTH, D_MODEL), F32),
        "ln2_b": 0.02 * nrm(ks[21], (DEPTH, D_MODEL), F32),
    }


def reference(x_prompt, x_sample, cache_k, cache_v, cache_kidx, page_table, state_conv, state_ffn_conv,
              w_in, conv_a, ln_cv_g, ln_cv_b, w_sp, b_sp, w_o, ln1_g, ln1_b, w_up, conv_f, w_down, ln2_g, ln2_b):
    Bp, Sp = x_prompt.shape[:2]
    Bd, Td = x_sample.shape[:2]
    past = page_table.shape[1] * PAGE_SIZE
    pos_p = jnp.arange(Sp, dtype=jnp.int32)
    pos_s = past + jnp.arange(Td, dtype=jnp.int32)
    hp, hs = x_prompt, x_sample
    kp_l, ks_l, vp_l, vs_l, kip_l, kis_l = [], [], [], [], [], []
    cp_l, cs_l, fp_l, fs_l, chs_l = [], [], [], [], []
    for l in range(DEPTH):
        gb, gc, ha, q, k, v, qi, ki, wi, u, vc = _project(hp, pos_p, w_in[l])
        a_out, conv_new = _short_conv(gb, gc, ha, jnp.zeros((Bp, CONV_W - 1, A_W), hp.dtype), conv_a[l])
        b_out = _dsa_prompt(q, k, v, qi, wi, ki)
        c_out, _ = _chunk_mlp(u, vc, ln_cv_g[l], ln_cv_b[l], w_sp[l], b_sp[l])
        mix = jnp.concatenate([a_out, b_out, c_out], axis=-1) @ w_o[l]
        hp = _layer_norm(ALPHA * hp + mix, ln1_g[l], ln1_b[l])
        f_out, ffn_new = _conv_ffn(hp, jnp.zeros((Bp, CONV_W - 1, 2 * D_FF), hp.dtype), w_up[l], conv_f[l], w_down[l])
        hp = _layer_norm(ALPHA * hp + f_out, ln2_g[l], ln2_b[l])
        kp_l.append(k); vp_l.append(v); kip_l.append(ki); cp_l.append(conv_new); fp_l.append(ffn_new)

        gb, gc, ha, q, k, v, qi, ki, wi, u, vc = _project(hs, pos_s, w_in[l])
        a_out, conv_new = _short_conv(gb, gc, ha, state_conv[l], conv_a[l])
        b_out = _dsa_sample(q, k, v, qi, wi, ki, cache_k[l], cache_v[l], cache_kidx[l], page_table)
        c_out, vn = _chunk_mlp(u, vc, ln_cv_g[l], ln_cv_b[l], w_sp[l], b_sp[l])
        mix = jnp.concatenate([a_out, b_out, c_out], axis=-1) @ w_o[l]
        hs = _layer_norm(ALPHA * hs + mix, ln1_g[l], ln1_b[l])
        f_out, ffn_new = _conv_ffn(hs, state_ffn_conv[l], w_up[l], conv_f[l], w_down[l])
        hs = _layer_norm(ALPHA * hs + f_out, ln2_g[l], ln2_b[l])
        ks_l.append(k); vs_l.append(v); kis_l.append(ki); cs_l.append(conv_new); fs_l.append(ffn_new); chs_l.append(vn)
    return (hp, hs,
            jnp.stack(kp_l), jnp.stack(ks_l),
            jnp.stack(vp_l), jnp.stack(vs_l),
            jnp.stack(kip_l), jnp.stack(kis_l),
            jnp.stack(cp_l), jnp.stack(cs_l),
            jnp.stack(fp_l), jnp.stack(fs_l),
            jnp.stack(chs_l))
```

```python
import sys
import numpy as np
from contextlib import ExitStack
import concourse.bass as bass
import concourse.mybir as mybir
from concourse.bass_utils import run_bass_kernel_spmd

F32 = mybir.dt.float32
BF16 = mybir.dt.bfloat16
I32 = mybir.dt.int32
AF = mybir.ActivationFunctionType
ALU = mybir.AluOpType

D = 1024
DFF = 2816
NCH_F = 44
ALPHA = 8.0 ** 0.25
LN_EPS = 1e-5
NIT = 22
KSEL = 256
NEG = -30000.0
WCOLS = 2696


class Buf:
    __slots__ = ("w", "r")

    def __init__(self):
        self.w = None
        self.r = {}


class Prog:
    CE = ("tensor", "vector", "scalar", "gpsimd")

    def __init__(self, nc, es, ndma=12):
        self.nc = nc
        self.q = {e: [] for e in ("tensor", "vector", "scalar", "gpsimd", "sync")}
        self.semh = {}
        self.cnt = {}
        self.waited = {e: {} for e in self.q}
        for e in self.CE:
            self.semh[e] = es.enter_context(nc.semaphore("s_" + e))
            self.cnt[e] = 0
        self.dsem = {}
        self.dcnt = {}
        self.dnext = {}
        for qn in ("sync", "gpsimd"):
            self.dsem[qn] = []
            for i in range(ndma):
                k = "d_%s_%d" % (qn, i)
                self.semh[k] = es.enter_context(nc.semaphore(k))
                self.dsem[qn].append(k)
                self.dcnt[k] = 0
            self.dnext[qn] = 0
        self.n = 0
        self.nops = 0

    def _waits(self, eng, reads, writes):
        need = {}
        for b in reads:
            if b.w is not None:
                s, v = b.w
                if need.get(s, 0) < v:
                    need[s] = v
        for b in writes:
            if b.w is not None:
                s, v = b.w
                if need.get(s, 0) < v:
                    need[s] = v
            for s, v in b.r.items():
                if need.get(s, 0) < v:
                    need[s] = v
        out = []
        wd = self.waited[eng]
        for s, v in need.items():
            if eng == "tensor" and s == "tensor":
                continue
            if wd.get(s, 0) >= v:
                continue
            wd[s] = v
            out.append((s, v))
        return out

    def _mark(self, ev, reads, writes):
        s, v = ev
        for b in reads:
            if b.r.get(s, 0) < v:
                b.r[s] = v
        for b in writes:
            b.w = ev
            b.r = {}

    LIMIT = None
    MARKS = []
    LOG = []

    def op(self, eng, reads, writes, fn):
        if Prog.LIMIT is not None and self.nops >= Prog.LIMIT:
            return
        self.nops += 1
        Prog.LOG.append((self.nops, eng, sys._getframe(2).f_lineno))
        waits = self._waits(eng, reads, writes)
        self.cnt[eng] += 1
        ev = (eng, self.cnt[eng])
        semh = self.semh
        sh = semh[eng]

        def run(e):
            for s, v in waits:
                e.wait_ge(semh[s], v)
            fn(e).then_inc(sh, 1)
        self.q[eng].append(run)
        self._mark(ev, reads, writes)
        self.n += 1 + len(waits)

    def dma(self, qn, reads, writes, fn):
        if Prog.LIMIT is not None and self.nops >= Prog.LIMIT:
            return
        self.nops += 1
        Prog.LOG.append((self.nops, 'dma-' + qn, sys._getframe(2).f_lineno))
        waits = self._waits(qn, reads, writes)
        i = self.dnext[qn]
        self.dnext[qn] = (i + 1) % len(self.dsem[qn])
        k = self.dsem[qn][i]
        prev = self.dcnt[k]
        if prev > 0 and self.waited[qn].get(k, 0) < 16 * prev:
            waits.append((k, 16 * prev))
            self.waited[qn][k] = 16 * prev
        self.dcnt[k] = prev + 1
        ev = (k, 16 * (prev + 1))
        semh = self.semh
        sh = semh[k]

        def run(e):
            for s, v in waits:
                e.wait_ge(semh[s], v)
            fn(e).then_inc(sh, 16)
        self.q[qn].append(run)
        self._mark(ev, reads, writes)
        self.n += 1 + len(waits)

    def finish(self):
        waits = [(k, 16 * c) for k, c in self.dcnt.items() if c > 0]
        semh = self.semh

        def run(e):
            for s, v in waits:
                e.wait_ge(semh[s], v)
        self.q["sync"].append(run)


def build(cfg):
    S, L, NB, NPG, NPOOL = cfg["S"], cfg["L"], cfg["NB"], cfg["NPG"], cfg["NPOOL"]
    PAST = NPG * 128
    NTP = 256
    NTS = NB * 4
    KMAX = max(S, PAST + 4)
    NKC = max(S // 128, NPG + 1)
    nc = bass.Bass("TRN2", target_bir_lowering=False)

    def din(name, shape, dt=F32):
        return nc.dram_tensor(name, list(shape), dt, kind="ExternalInput")

    def dout(name, shape, dt=F32):
        return nc.dram_tensor(name, list(shape), dt, kind="ExternalOutput")

    xT = din("xT", [D, S])
    xsT = din("xsT", [D, NTS])
    w_in = din("w_in", [L, D, WCOLS])
    w_o = din("w_o", [L, D, D])
    w_up = din("w_up", [L, D, 2 * DFF])
    w_dn = din("w_dn", [L, DFF, D])
    cva = din("cva", [128, L * 2 * 3])
    cvf = din("cvf", [128, L * NCH_F * 3])
    lng = din("lng", [128, L * 4 * 8])
    lncv = din("lncv", [128, L * 2 * 256])
    bspc = din("bspc", [128, L * 2 * 128])
    wspT = din("wspT", [128, L * 4 * 128])
    csp = din("csp", [S, 16])
    css = din("css", [4, 16])
    cident = din("cident", [128, 128])
    ctriu = din("ctriu", [128, 128])
    ctneg = din("ctneg", [128, 128])
    ciota = din("ciota", [128, 1])
    ck = din("ck", [L * NPOOL * 128, 128])
    cv = din("cv", [L * NPOOL * 128, 128])
    cki = din("cki", [L * NPOOL * 128, 64])
    pt = din("pt", [1, NB * NPG], I32)
    stcT = din("stcT", [L, 256, NB * 2])
    stfT = din("stfT", [L, 2 * DFF, NB * 2])

    y_p = dout("y_p", [S, D])
    y_s = dout("y_s", [NTS, D])
    nk_p = dout("nk_p", [L, S, 128])
    nk_s = dout("nk_s", [L, NTS, 128])
    nv_p = dout("nv_p", [L, S, 128])
    nv_s = dout("nv_s", [L, NTS, 128])
    nki_p = dout("nki_p", [L, S, 64])
    nki_s = dout("nki_s", [L, NTS, 64])
    ncv_p = dout("ncv_p", [L, 256, 2])
    ncv_s = dout("ncv_s", [L, 256, NB * 2])
    nff_p = dout("nff_p", [L, 2 * DFF, 2])
    nff_s = dout("nff_s", [L, 2 * DFF, NB * 2])
    nch_s = dout("nch_s", [L, NTS, 256])
    hsc_p = [nc.dram_tensor("hsc_p%d" % i, [D, S], F32, kind="Internal") for i in range(2)]
    hsc_s = [nc.dram_tensor("hsc_s%d" % i, [D, NTS], F32, kind="Internal") for i in range(2)]

    es = ExitStack()
    with es:
        P = Prog(nc, es)

        def sb(name, shape, dt=F32):
            return es.enter_context(nc.sbuf_tensor(name, list(shape), dt))

        ident = sb("ident", [128, 128]); identb = sb("identb", [128, 128], BF16)
        identrep = sb("identrep", [128, 4, 128], BF16)
        onesf = sb("onesf", [128, 128])
        triu = sb("triu", [128, 128]); tneg = sb("tneg", [128, 128])
        iota = sb("iota", [128, 1])
        cva_t = sb("cva_t", [128, L * 6]); cvf_t = sb("cvf_t", [128, L * NCH_F * 3])
        lng_t = sb("lng_t", [128, L * 32]); lncv_t = sb("lncv_t", [128, 512])
        bspc_t = sb("bspc_t", [128, 256])
        wmT = sb("wmT", [128, 4, 128], BF16)
        wtmp = sb("wtmp", [128, 128])
        idx = sb("idx", [128, NPG], I32)
        iol = sb("iol", [128, L])
        ptb = sb("ptb", [128, NPG], I32)
        kT = sb("kT", [128, KMAX], BF16)
        kiT = sb("kiT", [128, KMAX], BF16)
        Vp = sb("Vp", [128, NKC, 2, 65], BF16)
        sc = sb("sc", [128, max(KMAX, 11 * NTP)])
        junk = sb("junk", [128, 2048], mybir.dt.uint8)
        xt = sb("xt", [128, 8, NTP]); xb = sb("xb", [128, 8, NTP], BF16)
        fm = sb("fm", [128, 8, NTP])
        zt = sb("zt", [128, 2, NTP + 2 * NB])
        hl = sb("hl", [128, NCH_F, 2])
        stfc = [sb("stfc%d" % i, [128, NB * 2]) for i in range(4)]
        aT = sb("aT", [128, 2, NTP], BF16); cT = sb("cT", [128, 2, NTP], BF16)
        oT = sb("oT", [64, 2, 4, NTP], BF16)
        qT = sb("qT", [128, 4, 128], BF16); qiT = sb("qiT", [128, 4, 128], BF16)
        qq = sb("qq", [128, 1024]); qqb = sb("qqb", [128, 1024], BF16)
        kv = sb("kv", [128, 392]); kvb = sb("kvb", [128, 392], BF16)
        rt = sb("rt", [128, 4, 16, 8])
        cs = sb("cs", [128, 16])
        sm = sb("sm", [128, 64])
        Dm = sb("Dm", [128, 8, 128], BF16)
        vn = sb("vn", [128, 256]); vnb = sb("vnb", [128, 256], BF16)
        bst = sb("bst", [128, 8])
        rh = [sb("rh%d" % i, [128, 512], BF16) for i in range(4)]
        pT = [sb("pT%d" % i, [128, 512], BF16) for i in range(4)]
        nmc = [sb("nmc%d" % i, [128, 128], BF16) for i in range(4)]
        srow = sb("srow", [128, 512]); rs = sb("rs", [64, 512])
        winb = [sb("winb%d" % i, [128, 8, 256], BF16) for i in range(3)]
        wob = [sb("wob%d" % i, [128, 12, 128], BF16) for i in range(2)]
        wupb = [sb("wupb%d" % i, [128, 8, 2, 128], BF16) for i in range(2)]
        wdnb = [sb("wdnb%d" % i, [128, 11, 128], BF16) for i in range(3)]
        hs = [sb("hs%d" % i, [128, NTP + 2 * NB]) for i in range(4)]
        yv = [sb("yv%d" % i, [128, NTP]) for i in range(4)]
        mt = sb("mt", [128, NTP]); rt2 = sb("rt2", [128, NTP]); msq = sb("msq", [128, NTP])
        ctmp = sb("ctmp", [128, 128])
        kpg = [sb("kpg%d" % i, [128, 128]) for i in range(2)]
        vpg = [sb("vpg%d" % i, [128, 128]) for i in range(2)]
        ipg = [sb("ipg%d" % i, [128, 64]) for i in range(2)]
        kpb = [sb("kpb%d" % i, [128, 128], BF16) for i in range(2)]
        ipb = [sb("ipb%d" % i, [128, 128], BF16) for i in range(2)]
        pbank = [es.enter_context(nc.psum_tensor("pb%d" % i, [128, 512], F32)) for i in range(6)]
        pbank.append(es.enter_context(nc.psum_tensor("pb6", [128, 8, 128], BF16)))
        pbank.append(es.enter_context(nc.psum_tensor("pb7", [128, 8, 128], BF16)))
        B = {}

        def buf(o):
            if isinstance(o, Buf):
                return o
            k = o if isinstance(o, (tuple, str)) else id(o)
            if k not in B:
                B[k] = Buf()
            return B[k]
        Rn = [0]

        def R():
            i = Rn[0] % 4
            Rn[0] += 1
            return pbank[i]
        ptb_bf = pbank[7]
        pt2_bf = pbank[6]

        def V(r, w, fn): P.op("vector", [buf(x) for x in r], [buf(x) for x in w], fn)
        def A(r, w, fn): P.op("scalar", [buf(x) for x in r], [buf(x) for x in w], fn)
        def G(r, w, fn): P.op("gpsimd", [buf(x) for x in r], [buf(x) for x in w], fn)
        def T(r, w, fn): P.op("tensor", [buf(x) for x in r], [buf(x) for x in w], fn)
        def DS(r, w, fn): P.dma("sync", [buf(x) for x in r], [buf(x) for x in w], fn)
        def DG(r, w, fn): P.dma("gpsimd", [buf(x) for x in r], [buf(x) for x in w], fn)

        def dma_in(q, dst_t, dst_ap, src_ap, rd=()):
            (DS if q == "s" else DG)(list(rd), [dst_t], lambda e: e.dma_start(out=dst_ap, in_=src_ap))

        def dma_out(q, dst_ap, src_t, src_ap, wr=()):
            (DS if q == "s" else DG)([src_t], list(wr), lambda e: e.dma_start(out=dst_ap, in_=src_ap))

        dma_in("s", ident, ident[:, :], cident[:, :])
        dma_in("s", triu, triu[:, :], ctriu[:, :])
        dma_in("s", tneg, tneg[:, :], ctneg[:, :])
        dma_in("s", iota, iota[:, :], ciota[:, :])
        dma_in("s", cva_t, cva_t[:, :], cva[:, :])
        dma_in("s", cvf_t, cvf_t[:, :], cvf[:, :])
        dma_in("s", lng_t, lng_t[:, :], lng[:, :])
        V([ident], [identb], lambda e: e.tensor_copy(out=identb[:, :], in_=ident[:, :]))
        for r in range(4):
            V([ident], [identrep], lambda e, r=r: e.tensor_copy(out=identrep[:, r, :], in_=ident[:, :]))
        V([], [onesf], lambda e: e.memset(onesf[:, :], 1.0))
        V([], [Vp], lambda e: e.memset(Vp[:, :, :, 64:65], 1.0))
        for l_ in range(L):
            V([iota], [iol], lambda e, l_=l_: e.tensor_scalar(out=iol[:, l_:l_ + 1], in0=iota[:, 0:1], scalar1=float(l_ * NPOOL * 128),
                                                              scalar2=None, op0=ALU.add))

        wi_n = [0]
        wd_n = [0]

        def mm_group(out_t, out_ap, pairs, rd):
            n = len(pairs)
            for i, (l_, r_) in enumerate(pairs):
                T(rd, [out_t],
                  lambda e, l_=l_, r_=r_, i=i: e.matmul(out_ap, l_, r_, start=(i == 0), stop=(i == n - 1)))

        def process_sb(l, kind, sbi):
            Prog.MARKS.append(('sb', l, kind, sbi, P.nops))
            prm = kind == "p"
            NT = NTP if prm else NTS
            tok0 = sbi * NTP if prm else 0
            nq = 128 if prm else 4
            ntm = NT // nq
            if l == 0:
                src = xT if prm else xsT
            else:
                src = (hsc_p if prm else hsc_s)[(l - 1) % 2]
            srcb = ("h", prm, (l - 1) % 2, sbi)
            dma_in("s", xt, xt[:, :, 0:NT], src[:, tok0:tok0 + NT].rearrange("(c p) t -> p c t", p=128),
                   rd=[srcb] if l > 0 else [])
            G([xt], [xb], lambda e: e.tensor_copy(out=xb[:, :, 0:NT], in_=xt[:, :, 0:NT]))
            for half in range(4):
                wb = winb[wi_n[0] % 3]; wi_n[0] += 1
                dma_in("g", wb, wb[:, :, :], w_in[l, :, half * 256:(half + 1) * 256].rearrange("(k p) n -> p k n", p=128))
                for j in range(2):
                    oc = half * 2 + j
                    ps = R()
                    mm_group(ps, ps[:, 0:NT], [(wb[:, k, j * 128:(j + 1) * 128], xb[:, k, 0:NT]) for k in range(8)], [wb, xb])
                    A([ps], [fm], lambda e, ps=ps, oc=oc: e.activation(out=fm[:, oc, 0:NT], in_=ps[:, 0:NT], func=AF.Copy))
            if prm:
                zv = zt[:, :, 0:NT + 2].rearrange("p c (b t) -> p c b t", b=1)
                if sbi == 0:
                    G([], [zt], lambda e: e.memset(zt[:, :, 0:2], 0.0))
                else:
                    G([zt], [zt], lambda e: e.tensor_copy(out=zt[:, :, 0:2], in_=zt[:, :, NT:NT + 2]))
                nbk, tt = 1, NT
            else:
                zv = zt[:, :, 0:NB * 6].rearrange("p c (b t) -> p c b t", b=NB)
                for c_ in range(2):
                    DS([], [buf(zt)], lambda e, c_=c_: e.dma_start(
                        out=zv[:, c_, :, 0:2], in_=stcT[l, c_ * 128:(c_ + 1) * 128, :].rearrange("p (b i) -> p b i", i=2)))
                nbk, tt = NB, 4
            fmv = fm[:, :, 0:NT].rearrange("p c (b t) -> p c b t", b=nbk)
            V([fm, zt], [zt], lambda e: e.tensor_tensor(out=zv[:, :, :, 2:2 + tt], in0=fmv[:, 2:4, :, :],
                                                         in1=fmv[:, 4:6, :, :], op=ALU.mult))
            if prm and sbi == S // NTP - 1:
                dma_out("s", ncv_p[l, :, :].rearrange("(c p) i -> p c i", p=128), zt, zt[:, :, NT:NT + 2])
            if not prm:
                for c_ in range(2):
                    dma_out("s", ncv_s[l, c_ * 128:(c_ + 1) * 128, :].rearrange("p (b i) -> p b i", i=2), zt, zv[:, c_, :, 4:6])
            for c in range(2):
                yy = yv[c]
                yyv = yy[:, 0:NT].rearrange("p (b t) -> p b t", b=nbk)
                w = lambda i, c=c: cva_t[:, (l * 2 + c) * 3 + i:(l * 2 + c) * 3 + i + 1]
                V([zt, cva_t], [yy], lambda e, c=c, yyv=yyv, w=w: e.tensor_scalar(
                    out=yyv, in0=zv[:, c, :, 2:2 + tt], scalar1=w(2), scalar2=None, op0=ALU.mult))
                V([zt, yy], [yy], lambda e, c=c, yyv=yyv, w=w: e.scalar_tensor_tensor(
                    out=yyv, in0=zv[:, c, :, 1:1 + tt], scalar=w(1), in1=yyv, op0=ALU.mult, op1=ALU.add))
                V([zt, yy], [yy], lambda e, c=c, yyv=yyv, w=w: e.scalar_tensor_tensor(
                    out=yyv, in0=zv[:, c, :, 0:tt], scalar=w(0), in1=yyv, op0=ALU.mult, op1=ALU.add))
                V([yy, fm], [aT], lambda e, c=c, yy=yy: e.tensor_tensor(
                    out=aT[:, c, 0:NT], in0=yy[:, 0:NT], in1=fm[:, c, 0:NT], op=ALU.mult))
            for tm in range(ntm):
                t0 = tm * nq
                g0 = tok0 + t0
                if prm:
                    jblk = g0 // 128
                    koff = jblk * 128
                    vchunk = jblk
                    nkeys = koff + 128
                else:
                    koff = PAST
                    vchunk = NPG
                    nkeys = PAST + 4
                    Prog.MARKS.append(('pages', l, tm, P.nops))
                    stage_pages(l, tm)
                tm_block(l, prm, NT, nq, t0, g0, koff, vchunk, nkeys)
            for oc in range(8):
                wb = wob[oc % 2]
                dma_in("g", wb, wb[:, 0:2, :], w_o[l, 0:256, oc * 128:(oc + 1) * 128].rearrange("(c p) n -> p c n", p=128))
                dma_in("g", wb, wb[0:64, 2:10, :], w_o[l, 256:768, oc * 128:(oc + 1) * 128].rearrange("(h p) n -> p h n", p=64))
                dma_in("g", wb, wb[:, 10:12, :], w_o[l, 768:1024, oc * 128:(oc + 1) * 128].rearrange("(c p) n -> p c n", p=128))
                pairs = [(wb[:, c, :], aT[:, c, 0:NT]) for c in range(2)]
                pairs += [(wb[0:64, 2 + h, :], oT[0:64, h // 4, h % 4, 0:NT]) for h in range(8)]
                pairs += [(wb[:, 10 + c, :], cT[:, c, 0:NT]) for c in range(2)]
                ps = R()
                mm_group(ps, ps[:, 0:NT], pairs, [wb, aT, oT, cT])
                V([ps, xt], [xt], lambda e, ps=ps, oc=oc: e.scalar_tensor_tensor(
                    out=xt[:, oc, 0:NT], in0=xt[:, oc, 0:NT], scalar=ALPHA, in1=ps[:, 0:NT], op0=ALU.mult, op1=ALU.add))
            layer_norm(l, 0, NT)
            G([xt], [xb], lambda e: e.tensor_copy(out=xb[:, :, 0:NT], in_=xt[:, :, 0:NT]))
            actT = sc[:, 0:11 * NTP].bitcast(BF16)
            hview = lambda t_: t_[:, 0:(NT + 2 * nbk)].rearrange("p (b t) -> p b t", b=nbk)
            for i in range(22):
                wb = wupb[i % 2]
                dma_in("g", wb, wb[:, :, 0, :], w_up[l, :, i * 128:(i + 1) * 128].rearrange("(k p) n -> p k n", p=128))
                dma_in("g", wb, wb[:, :, 1, :], w_up[l, :, DFF + i * 128:DFF + (i + 1) * 128].rearrange("(k p) n -> p k n", p=128))
                ys = []
                for gu in range(2):
                    ch = gu * 22 + i
                    ps = R()
                    mm_group(ps, ps[:, 0:NT], [(wb[:, k, gu, :], xb[:, k, 0:NT]) for k in range(8)], [wb, xb])
                    h_ = hs[(2 * i + gu) % 4]
                    hv = hview(h_)
                    if prm:
                        if sbi == 0:
                            G([], [h_], lambda e, hv=hv: e.memset(hv[:, :, 0:2], 0.0))
                        else:
                            G([hl], [h_], lambda e, hv=hv, ch=ch: e.tensor_copy(out=hv[:, 0, 0:2], in_=hl[:, ch, :]))
                    else:
                        st_ = stfc[(2 * i + gu) % 4]
                        dma_in("s", st_, st_[:, :], stfT[l, ch * 128:(ch + 1) * 128, :])
                        G([st_], [h_], lambda e, hv=hv, st_=st_: e.tensor_copy(
                            out=hv[:, :, 0:2], in_=st_[:, :].rearrange("p (b i) -> p b i", i=2)))
                    A([ps], [h_], lambda e, hv=hv, ps=ps: e.activation(
                        out=hv[:, :, 2:2 + tt], in_=ps[:, 0:NT].rearrange("p (b t) -> p b t", b=nbk), func=AF.Copy))
                    if prm:
                        G([h_], [hl], lambda e, hv=hv, ch=ch: e.tensor_copy(out=hl[:, ch, :], in_=hv[:, 0, NT:NT + 2]))
                    else:
                        dma_out("s", nff_s[l, ch * 128:(ch + 1) * 128, :].rearrange("p (b i) -> p b i", i=2), h_, hv[:, :, 4:6])
                    yy = yv[(2 * i + gu) % 4]
                    yyv = yy[:, 0:NT].rearrange("p (b t) -> p b t", b=nbk)
                    w = lambda k_, ch=ch: cvf_t[:, (l * NCH_F + ch) * 3 + k_:(l * NCH_F + ch) * 3 + k_ + 1]
                    G([h_, cvf_t], [yy], lambda e, hv=hv, yyv=yyv, w=w: e.tensor_scalar(
                        out=yyv, in0=hv[:, :, 2:2 + tt], scalar1=w(2), scalar2=None, op0=ALU.mult))
                    V([h_, yy], [yy], lambda e, hv=hv, yyv=yyv, w=w: e.scalar_tensor_tensor(
                        out=yyv, in0=hv[:, :, 1:1 + tt], scalar=w(1), in1=yyv, op0=ALU.mult, op1=ALU.add))
                    V([h_, yy], [yy], lambda e, hv=hv, yyv=yyv, w=w: e.scalar_tensor_tensor(
                        out=yyv, in0=hv[:, :, 0:tt], scalar=w(0), in1=yyv, op0=ALU.mult, op1=ALU.add))
                    ys.append(yy)
                A([ys[0]], [ys[0]], lambda e, y0=ys[0]: e.activation(out=y0[:, 0:NT], in_=y0[:, 0:NT], func=AF.Silu))
                V([ys[0], ys[1]], [sc], lambda e, i=i, y0=ys[0], y1=ys[1]: e.tensor_tensor(
                    out=actT[:, i * NT:(i + 1) * NT], in0=y0[:, 0:NT], in1=y1[:, 0:NT], op=ALU.mult))
            if prm and sbi == S // NTP - 1:
                dma_out("s", nff_p[l, :, :].rearrange("(c p) i -> p c i", p=128), hl, hl[:, :, :])
            for oc in range(8):
                ps = R()
                for hf in range(2):
                    wb = wdnb[wd_n[0] % 3]; wd_n[0] += 1
                    dma_in("g", wb, wb[:, :, :], w_dn[l, hf * 1408:(hf + 1) * 1408, oc * 128:(oc + 1) * 128].rearrange("(i p) n -> p i n", p=128))
                    for i in range(11):
                        ii = hf * 11 + i
                        T([wb, sc], [ps], lambda e, ps=ps, wb=wb, i=i, ii=ii: e.matmul(
                            ps[:, 0:NT], wb[:, i, :], actT[:, ii * NT:(ii + 1) * NT], start=(ii == 0), stop=(ii == 21)))
                V([ps, xt], [xt], lambda e, ps=ps, oc=oc: e.scalar_tensor_tensor(
                    out=xt[:, oc, 0:NT], in0=xt[:, oc, 0:NT], scalar=ALPHA, in1=ps[:, 0:NT], op0=ALU.mult, op1=ALU.add))
            layer_norm(l, 1, NT)
            if l < L - 1:
                dst = (hsc_p if prm else hsc_s)[l % 2]
                dma_out("s", dst[:, tok0:tok0 + NT].rearrange("(c p) t -> p c t", p=128), xt, xt[:, :, 0:NT],
                        wr=[("h", prm, l % 2, sbi)])
            else:
                nb_ = max(1, NT // 128)
                tw = min(NT, 128)
                for b_ in range(nb_):
                    for c in range(8):
                        ps = R()
                        T([xt, ident], [ps], lambda e, ps=ps, c=c, b_=b_: e.transpose(
                            ps[0:tw, 0:128], xt[:, c, b_ * tw:(b_ + 1) * tw], ident[:, :]))
                        A([ps], [sc], lambda e, ps=ps, c=c: e.activation(
                            out=sc[0:tw, c * 128:(c + 1) * 128], in_=ps[0:tw, 0:128], func=AF.Copy))
                    ydst = y_p if prm else y_s
                    dma_out("s", ydst[tok0 + b_ * tw:tok0 + (b_ + 1) * tw, :], sc, sc[0:tw, 0:1024])

        def layer_norm(l, which, NT):
            sq = sc[:, 0:8 * NTP].rearrange("p (c t) -> p c t", c=8)
            A([xt], [sc], lambda e: e.activation(out=sq[:, :, 0:NT], in_=xt[:, :, 0:NT], func=AF.Square))
            p1 = R(); p2 = R()
            mm_group(p1, p1[:, 0:NT], [(onesf[:, :], xt[:, c, 0:NT]) for c in range(8)], [onesf, xt])
            mm_group(p2, p2[:, 0:NT], [(onesf[:, :], sq[:, c, 0:NT]) for c in range(8)], [onesf, sc])
            A([p1], [mt], lambda e: e.mul(out=mt[:, 0:NT], in_=p1[:, 0:NT], mul=1.0 / D))
            V([mt], [msq], lambda e: e.tensor_tensor(out=msq[:, 0:NT], in0=mt[:, 0:NT], in1=mt[:, 0:NT], op=ALU.mult))
            V([p2, msq], [rt2], lambda e: e.scalar_tensor_tensor(
                out=rt2[:, 0:NT], in0=p2[:, 0:NT], scalar=1.0 / D, in1=msq[:, 0:NT], op0=ALU.mult, op1=ALU.subtract))
            V([rt2], [rt2], lambda e: e.tensor_scalar(out=rt2[:, 0:NT], in0=rt2[:, 0:NT], scalar1=LN_EPS, scalar2=None, op0=ALU.add))
            A([rt2], [rt2], lambda e: e.activation(out=rt2[:, 0:NT], in_=rt2[:, 0:NT], func=AF.Sqrt))
            V([rt2], [rt2], lambda e: e.reciprocal(out=rt2[:, 0:NT], in_=rt2[:, 0:NT]))
            V([xt, mt], [xt], lambda e: e.tensor_tensor(
                out=xt[:, :, 0:NT], in0=xt[:, :, 0:NT], in1=mt[:, 0:NT].unsqueeze(1).to_broadcast([128, 8, NT]), op=ALU.subtract))
            V([xt, rt2], [xt], lambda e: e.tensor_tensor(
                out=xt[:, :, 0:NT], in0=xt[:, :, 0:NT], in1=rt2[:, 0:NT].unsqueeze(1).to_broadcast([128, 8, NT]), op=ALU.mult))
            for c in range(8):
                gi = l * 32 + (2 * which) * 8 + c
                bi = l * 32 + (2 * which + 1) * 8 + c
                eng = V if c % 2 == 0 else G
                eng([xt, lng_t], [xt], lambda e, c=c, gi=gi, bi=bi: e.tensor_scalar(
                    out=xt[:, c, 0:NT], in0=xt[:, c, 0:NT], scalar1=lng_t[:, gi:gi + 1], scalar2=lng_t[:, bi:bi + 1],
                    op0=ALU.mult, op1=ALU.add))

        def rotary(tile_t, ap3, H, nq):
            x1 = ap3[:, :, 0:8]; x2 = ap3[:, :, 8:16]
            cb = cs[0:nq, 0:8].unsqueeze(1).to_broadcast([nq, H, 8])
            sb_ = cs[0:nq, 8:16].unsqueeze(1).to_broadcast([nq, H, 8])
            t = [rt[0:nq, i, 0:H, :] for i in range(4)]
            V([tile_t, cs], [rt], lambda e: e.tensor_tensor(out=t[0], in0=x1, in1=cb, op=ALU.mult))
            V([tile_t, cs], [rt], lambda e: e.tensor_tensor(out=t[1], in0=x2, in1=sb_, op=ALU.mult))
            V([tile_t, cs], [rt], lambda e: e.tensor_tensor(out=t[2], in0=x2, in1=cb, op=ALU.mult))
            V([tile_t, cs], [rt], lambda e: e.tensor_tensor(out=t[3], in0=x1, in1=sb_, op=ALU.mult))
            V([rt], [tile_t], lambda e: e.tensor_tensor(out=x1, in0=t[0], in1=t[1], op=ALU.subtract))
            V([rt], [tile_t], lambda e: e.tensor_tensor(out=x2, in0=t[2], in1=t[3], op=ALU.add))

        pg_n = [0]

        def stage_pages(l, b):
            dma_in("s", ptb, ptb[:, :], pt[0:1, b * NPG:(b + 1) * NPG].to_broadcast([128, NPG]))
            V([ptb, iol], [idx], lambda e: e.tensor_scalar(out=idx[:, :], in0=ptb[:, :], scalar1=128.0,
                                                           scalar2=iol[:, l:l + 1], op0=ALU.mult, op1=ALU.add))
            for j in range(NPG):
                s_ = pg_n[0] % 2; pg_n[0] += 1
                ia = idx[:, j:j + 1]
                for (dst, srcT) in ((kpg[s_], ck), (vpg[s_], cv), (ipg[s_], cki)):
                    DG([idx], [buf(dst)], lambda e, dst=dst, srcT=srcT, ia=ia: e.indirect_dma_start(
                        out=dst[:, :], out_offset=None, in_=srcT[:, :],
                        in_offset=bass.IndirectOffsetOnAxis(ap=ia, axis=0)))
                G([kpg[s_]], [kpb[s_]], lambda e, s_=s_: e.tensor_copy(out=kpb[s_][:, :], in_=kpg[s_][:, :]))
                G([ipg[s_]], [ipb[s_]], lambda e, s_=s_: e.tensor_copy(
                    out=ipb[s_][:, :].rearrange("p (a d) -> p a d", a=2), in_=ipg[s_][:, :].unsqueeze(1).to_broadcast([128, 2, 64])))
                G([vpg[s_]], [Vp], lambda e, s_=s_, j=j: e.tensor_copy(
                    out=Vp[:, j, :, 0:64], in_=vpg[s_][:, :].rearrange("p (g d) -> p g d", g=2)))
                T([kpb[s_], identb], [pbank[7]], lambda e, s_=s_: e.transpose(ptb_bf[:, 0, :], kpb[s_][:, :], identb[:, :]))
                T([ipb[s_], identb], [pbank[7]], lambda e, s_=s_: e.transpose(ptb_bf[:, 1, :], ipb[s_][:, :], identb[:, :]))
                A([pbank[7]], [kT], lambda e, j=j: e.activation(out=kT[:, j * 128:(j + 1) * 128], in_=ptb_bf[:, 0, :], func=AF.Copy))
                A([pbank[7]], [kiT], lambda e, j=j: e.activation(out=kiT[:, j * 128:(j + 1) * 128], in_=ptb_bf[:, 1, :], func=AF.Copy))

        def tm_block(l, prm, NT, nq, t0, g0, koff, vchunk, nkeys):
            Prog.MARKS.append(('tm', l, prm, g0, P.nops))
            if prm:
                dma_in("s", cs, cs[0:nq, :], csp[g0:g0 + nq, :])
            else:
                dma_in("s", cs, cs[0:nq, :], css[0:nq, :])
            xcols = lambda k: xb[:, k, t0:t0 + nq]
            groups = [(1024, 512), (1536, 512), (2048, 392), (2440, 256)]
            pss = []
            for gi, (c0, cw) in enumerate(groups):
                ps = R()
                for p0 in range(0, cw, 256):
                    pw = min(256, cw - p0)
                    wb = winb[wi_n[0] % 3]; wi_n[0] += 1
                    dma_in("g", wb, wb[:, :, 0:pw], w_in[l, :, c0 + p0:c0 + p0 + pw].rearrange("(k p) n -> p k n", p=128))
                    mm_group(ps, ps[0:nq, p0:p0 + pw], [(xcols(k), wb[:, k, 0:pw]) for k in range(8)], [wb, xb])
                if gi == 0:
                    A([ps], [qq], lambda e, ps=ps: e.activation(out=qq[0:nq, 0:512], in_=ps[0:nq, 0:512], func=AF.Copy))
                elif gi == 1:
                    A([ps], [qq], lambda e, ps=ps: e.activation(out=qq[0:nq, 512:1024], in_=ps[0:nq, 0:512], func=AF.Copy))
                elif gi == 2:
                    A([ps], [kv], lambda e, ps=ps: e.activation(out=kv[0:nq, :], in_=ps[0:nq, 0:392], func=AF.Copy))
                else:
                    psvc = ps
            V([psvc], [bst], lambda e: e.bn_stats(out=bst[0:nq, 0:6], in_=psvc[0:nq, 0:256]))
            V([bst], [sm], lambda e: e.bn_aggr(out=sm[0:nq, 0:2], in_=bst[0:nq, 0:6]))
            V([sm], [sm], lambda e: e.tensor_scalar(out=sm[0:nq, 2:3], in0=sm[0:nq, 1:2], scalar1=LN_EPS, scalar2=None, op0=ALU.add))
            A([sm], [sm], lambda e: e.activation(out=sm[0:nq, 3:4], in_=sm[0:nq, 2:3], func=AF.Sqrt))
            V([sm], [sm], lambda e: e.reciprocal(out=sm[0:nq, 4:5], in_=sm[0:nq, 3:4]))
            V([psvc, sm], [vn], lambda e: e.tensor_scalar(out=vn[0:nq, :], in0=psvc[0:nq, 0:256], scalar1=sm[0:nq, 0:1],
                                                          scalar2=sm[0:nq, 4:5], op0=ALU.subtract, op1=ALU.mult))
            V([vn, lncv_t], [vn], lambda e: e.tensor_tensor(out=vn[0:nq, :], in0=vn[0:nq, :],
                                                            in1=lncv_t[0:nq, 0:256], op=ALU.mult))
            V([vn, lncv_t], [vn], lambda e: e.tensor_tensor(out=vn[0:nq, :], in0=vn[0:nq, :],
                                                            in1=lncv_t[0:nq, 256:512], op=ALU.add))
            G([vn], [vnb], lambda e: e.tensor_copy(out=vnb[0:nq, :], in_=vn[0:nq, :]))
            if not prm:
                dma_out("s", nch_s[l, g0:g0 + nq, :], vn, vn[0:nq, :])
            rotary(qq, qq[0:nq, :].rearrange("p (h d) -> p h d", d=64), 16, nq)
            rotary(kv, kv[0:nq, 0:128].rearrange("p (h d) -> p h d", d=64), 2, nq)
            rotary(kv, kv[0:nq, 256:384].rearrange("p (h d) -> p h d", d=64), 2, nq)
            ok, ov, oki = (nk_p, nv_p, nki_p) if prm else (nk_s, nv_s, nki_s)
            dma_out("s", ok[l, g0:g0 + nq, :], kv, kv[0:nq, 0:128])
            dma_out("s", ov[l, g0:g0 + nq, :], kv, kv[0:nq, 128:256])
            dma_out("s", oki[l, g0:g0 + nq, :], kv, kv[0:nq, 256:320])
            V([kv], [sm], lambda e: e.tensor_scalar(out=sm[0:nq, 8:16], in0=kv[0:nq, 384:392], scalar1=0.0, scalar2=2.0,
                                                    op0=ALU.is_gt, op1=ALU.mult))
            V([sm], [sm], lambda e: e.tensor_scalar(out=sm[0:nq, 8:16], in0=sm[0:nq, 8:16], scalar1=-1.0, scalar2=None, op0=ALU.add))
            V([kv, sm], [sm], lambda e: e.scalar_tensor_tensor(out=sm[0:nq, 16:24], in0=kv[0:nq, 384:392],
                                                               scalar=float(8.0 ** -0.5 * 0.125), in1=sm[0:nq, 8:16],
                                                               op0=ALU.mult, op1=ALU.mult))
            V([qq, sm], [qqb], lambda e: e.tensor_tensor(
                out=qqb[0:nq, 512:1024].rearrange("p (h d) -> p h d", d=64),
                in0=qq[0:nq, 512:1024].rearrange("p (h d) -> p h d", d=64),
                in1=sm[0:nq, 16:24].unsqueeze(2).to_broadcast([nq, 8, 64]), op=ALU.mult))
            A([qq], [qqb], lambda e: e.activation(out=qqb[0:nq, 0:512], in_=qq[0:nq, 0:512], func=AF.Copy))
            G([kv], [kvb], lambda e: e.tensor_copy(out=kvb[0:nq, :], in_=kv[0:nq, :]))
            G([kv], [Vp], lambda e: e.tensor_copy(out=Vp[0:nq, vchunk, :, 0:64],
                                                  in_=kv[0:nq, 128:256].rearrange("p (g d) -> p g d", g=2)))
            for h in range(8):
                G([identb, sm], [Dm], lambda e, h=h: e.tensor_scalar(
                    out=Dm[0:nq, h, 0:nq], in0=identb[0:nq, 0:nq], scalar1=sm[0:nq, 8 + h:9 + h], scalar2=None, op0=ALU.mult))
            for r in range(8):
                T([qqb, identb], [pbank[7]], lambda e, r=r: e.transpose(
                    ptb_bf[:, r, 0:nq], qqb[0:nq, r * 128:(r + 1) * 128], identb[0:nq, 0:nq]))
            A([pbank[7]], [qT], lambda e: e.activation(out=qT[:, :, 0:nq], in_=ptb_bf[:, 0:4, 0:nq], func=AF.Copy))
            A([pbank[7]], [qiT], lambda e: e.activation(out=qiT[:, :, 0:nq], in_=ptb_bf[:, 4:8, 0:nq], func=AF.Copy))
            T([kvb, identb], [pbank[6]], lambda e: e.transpose(pt2_bf[:, 0, 0:nq], kvb[0:nq, 0:128], identb[0:nq, 0:nq]))
            T([kvb, identb], [pbank[6]], lambda e: e.transpose(pt2_bf[:, 1, 0:nq], kvb[0:nq, 256:384], identb[0:nq, 0:nq]))
            A([pbank[6]], [kT], lambda e: e.activation(out=kT[:, koff:koff + nq], in_=pt2_bf[:, 0, 0:nq], func=AF.Copy))
            A([pbank[6]], [kiT], lambda e: e.activation(out=kiT[:, koff:koff + nq], in_=pt2_bf[:, 1, 0:nq], func=AF.Copy))
            cps = R()
            for c in range(2):
                for hh in range(2):
                    h = 2 * c + hh
                    T([vnb, wmT], [cps], lambda e, c=c, hh=hh, h=h: e.matmul(
                        cps[64 * hh:64 * hh + 64, c * nq:(c + 1) * nq], vnb[0:nq, h * 64:(h + 1) * 64], wmT[0:nq, h, 0:nq],
                        start=True, stop=True))
            for c in range(2):
                V([cps, bspc_t], [ctmp], lambda e, c=c: e.tensor_tensor(
                    out=ctmp[:, 0:nq], in0=cps[:, c * nq:(c + 1) * nq],
                    in1=bspc_t[:, c * 128:c * 128 + nq], op=ALU.add))
                V([ctmp, fm], [cT], lambda e, c=c: e.tensor_tensor(
                    out=cT[:, c, t0:t0 + nq], in0=ctmp[:, 0:nq], in1=fm[:, 6 + c, t0:t0 + nq], op=ALU.mult))
            chunks = [(c0, min(512, nkeys - c0)) for c0 in range(0, nkeys, 512)]
            steps = [(ci, h) for ci in range(len(chunks)) for h in range(8)]
            pend = []

            def fin(i):
                ci, h = steps[i]
                c0, w = chunks[ci]
                ps, rr = pend[i]
                acc = pbank[4 + ci % 2]
                A([ps], [rr], lambda e: e.activation(out=rr[0:nq, 0:w], in_=ps[0:nq, 0:w], func=AF.Relu))
                T([Dm, rr], [acc], lambda e: e.matmul(acc[0:nq, 0:w], Dm[0:nq, h, 0:nq], rr[0:nq, 0:w],
                                                     start=(h == 0), stop=(h == 7)))
                if h == 7:
                    A([acc], [sc], lambda e: e.activation(out=sc[0:nq, c0:c0 + w], in_=acc[0:nq, 0:w], func=AF.Copy))
            for i, (ci, h) in enumerate(steps):
                c0, w = chunks[ci]
                ps = R(); rr = rh[i % 4]
                pend.append((ps, rr))
                hp = 64 * (h % 2)
                T([qiT, kiT], [ps], lambda e, ps=ps, hp=hp, h=h, c0=c0, w=w: e.matmul(
                    ps[0:nq, 0:w], qiT[hp:hp + 64, h // 2, 0:nq], kiT[hp:hp + 64, c0:c0 + w], start=True, stop=True))
                if i >= 2:
                    fin(i - 2)
            for i in range(max(0, len(steps) - 2), len(steps)):
                fin(i)
            ksel = min(256, (S if prm else PAST + 4) // 4)
            V([sc], [sm], lambda e: e.tensor_reduce(out=sm[0:nq, 24:25], in_=sc[0:nq, 0:nkeys], axis=mybir.AxisListType.X, op=ALU.max))
            V([sc], [sm], lambda e: e.tensor_reduce(out=sm[0:nq, 25:26], in_=sc[0:nq, 0:nkeys], axis=mybir.AxisListType.X, op=ALU.min))
            V([sc, tneg], [sc], lambda e: e.tensor_tensor(out=sc[0:nq, koff:koff + nq], in0=sc[0:nq, koff:koff + nq],
                                                          in1=tneg[0:nq, 0:nq], op=ALU.add))
            V([sm], [sm], lambda e: e.scalar_tensor_tensor(out=sm[0:nq, 26:27], in0=sm[0:nq, 24:25], scalar=1.0,
                                                           in1=sm[0:nq, 25:26], op0=ALU.add, op1=ALU.subtract))
            lo = sm[0:nq, 25:26]; w0 = sm[0:nq, 26:27]; mid = sm[0:nq, 27:28]; cnt = sm[0:nq, 28:29]; tmp = sm[0:nq, 29:30]
            for it in range(NIT):
                cc = 2.0 ** -(it + 1)
                V([sm], [sm], lambda e, cc=cc: e.scalar_tensor_tensor(out=mid, in0=w0, scalar=cc, in1=lo, op0=ALU.mult, op1=ALU.add))
                for s0 in range(0, nkeys, 2048):
                    sw = min(2048, nkeys - s0)
                    V([sc, sm], [junk, sm], lambda e, s0=s0, sw=sw: e.tensor_scalar(
                        out=junk[0:nq, 0:sw], in0=sc[0:nq, s0:s0 + sw], scalar1=mid, scalar2=(None if s0 == 0 else cnt),
                        op0=ALU.is_ge, op1=ALU.add, accum_out=cnt))
                V([sm], [sm], lambda e: e.tensor_scalar(out=tmp, in0=cnt, scalar1=ksel - 0.5, scalar2=w0, op0=ALU.is_ge, op1=ALU.mult))
                V([sm], [sm], lambda e, cc=cc: e.scalar_tensor_tensor(out=lo, in0=tmp, scalar=cc, in1=lo, op0=ALU.mult, op1=ALU.add))
            if prm:
                kcs = [(kc, kc * 128, 128) for kc in range(koff // 128 + 1)]
            else:
                kcs = [(kc, kc * 128, 128) for kc in range(NPG)] + [(NPG, PAST, 4)]
            og = [pbank[4], pbank[5]]
            N4 = 4 * nq
            asteps = [(a, g) for a in range(len(kcs)) for g in range(2)]
            apend = []

            def afin(i):
                a, g = asteps[i]
                kc, k0, kw = kcs[a]
                st, pt_ = apend[i]
                A([st], [pt_], lambda e: e.activation(out=pt_[0:kw, 0:N4], in_=st[0:kw, 0:N4], func=AF.Exp, scale=0.125))
                T([Vp, pt_], [og[g]], lambda e: e.matmul(og[g][0:65, 0:N4], Vp[0:kw, kc, g, 0:65], pt_[0:kw, 0:N4],
                                                        start=(a == 0), stop=(a == len(kcs) - 1)))
            for i, (a, g) in enumerate(asteps):
                kc, k0, kw = kcs[a]
                if g == 0:
                    nm_ = nmc[a % 4]
                    V([sc, sm], [nm_], lambda e, nm_=nm_, k0=k0, kw=kw: e.tensor_scalar(
                        out=nm_[0:nq, 0:kw], in0=sc[0:nq, k0:k0 + kw], scalar1=lo, scalar2=NEG, op0=ALU.is_lt, op1=ALU.mult))
                st = R(); pt_ = pT[i % 4]
                apend.append((st, pt_))
                st3 = st[0:kw, 0:N4].rearrange("p (r t) -> p r t", r=4)
                T([kT, qT], [st], lambda e, st3=st3, g=g, k0=k0, kw=kw: e.matmul(
                    st3, kT[64 * g:64 * g + 64, k0:k0 + kw], qT[64 * g:64 * g + 64, :, 0:nq], start=True, stop=False))
                T([nm_, identrep], [st], lambda e, st3=st3, nm_=nm_, kw=kw: e.matmul(
                    st3, nm_[0:nq, 0:kw], identrep[0:nq, :, 0:nq], start=False, stop=True))
                if i >= 2:
                    afin(i - 2)
            for i in range(max(0, len(asteps) - 2), len(asteps)):
                afin(i)
            for g in range(2):
                A([og[g]], [srow], lambda e, g=g: e.activation(out=srow[64:65, 0:N4], in_=og[g][64:65, 0:N4], func=AF.Copy))
                pbs = R()
                T([onesf, srow], [pbs], lambda e, pbs=pbs: e.matmul(pbs[0:64, 0:N4], onesf[64:65, 0:64], srow[64:65, 0:N4],
                                                                   start=True, stop=True))
                V([pbs], [rs], lambda e, pbs=pbs: e.reciprocal(out=rs[0:64, 0:N4], in_=pbs[0:64, 0:N4]))
                V([og[g], rs], [oT], lambda e, g=g: e.tensor_tensor(
                    out=oT[0:64, g, :, t0:t0 + nq], in0=og[g][0:64, 0:N4].rearrange("p (r t) -> p r t", r=4),
                    in1=rs[0:64, 0:N4].rearrange("p (r t) -> p r t", r=4), op=ALU.mult))

        for l in range(L):
            dma_in("s", lncv_t, lncv_t[:, :], lncv[:, l * 512:(l + 1) * 512])
            dma_in("s", bspc_t, bspc_t[:, :], bspc[:, l * 256:(l + 1) * 256])
            dma_in("s", wtmp, wtmp[:, :], wspT[:, l * 512:l * 512 + 128])
            for h in range(4):
                if h > 0:
                    dma_in("s", wtmp, wtmp[:, :], wspT[:, (l * 4 + h) * 128:(l * 4 + h + 1) * 128])
                V([wtmp, triu], [wmT], lambda e, h=h: e.tensor_tensor(out=wmT[:, h, :], in0=wtmp[:, :], in1=triu[:, :], op=ALU.mult))
            for sbi in range(S // NTP):
                process_sb(l, "p", sbi)
            process_sb(l, "s", 0)
        P.finish()

        with nc.Block() as block:
            @block.tensor
            def _(e):
                for f in P.q["tensor"]:
                    f(e)

            @block.vector
            def _(e):
                for f in P.q["vector"]:
                    f(e)

            @block.scalar
            def _(e):
                for f in P.q["scalar"]:
                    f(e)

            @block.gpsimd
            def _(e):
                for f in P.q["gpsimd"]:
                    f(e)

            @block.sync
            def _(e):
                for f in P.q["sync"]:
                    f(e)
    return nc, P.n


def _host_layout(inputs, cfg, core):
    S, L, NB, NPG, NPOOL = cfg["S"], cfg["L"], cfg["NB"], cfg["NPG"], cfg["NPOOL"]
    f = lambda a: np.ascontiguousarray(a, dtype=np.float32)
    w_in = np.asarray(inputs["w_in"])
    qperm = np.concatenate([768 + h * 64 + np.arange(64) for h in (0, 4, 1, 5, 2, 6, 3, 7)])
    r = np.arange
    cols = np.concatenate([r(0, 256), r(256, 512), r(512, 768), r(2120, 2376), qperm, r(1536, 2048),
                           r(1280, 1408), r(1408, 1536), r(2048, 2112), r(2048, 2112), r(2112, 2120), r(2376, 2632)])
    assert cols.size == WCOLS
    m = {}
    m["xT"] = f(np.asarray(inputs["x_prompt"])[core].T)
    xs = np.asarray(inputs["x_sample"])[core * NB:(core + 1) * NB].reshape(NB * 4, D)
    m["xsT"] = f(xs.T)
    m["w_in"] = f(w_in[:, :, cols])
    m["w_o"] = f(inputs["w_o"]); m["w_up"] = f(inputs["w_up"]); m["w_dn"] = f(inputs["w_down"])
    ca = np.asarray(inputs["conv_a"])
    m["cva"] = f(ca.reshape(L, 3, 2, 128).transpose(3, 0, 2, 1).reshape(128, L * 6))
    cf = np.asarray(inputs["conv_f"])
    m["cvf"] = f(cf.reshape(L, 3, NCH_F, 128).transpose(3, 0, 2, 1).reshape(128, L * NCH_F * 3))
    ln = np.stack([np.asarray(inputs[k]) for k in ("ln1_g", "ln1_b", "ln2_g", "ln2_b")], axis=1)
    m["lng"] = f(ln.reshape(L, 4, 8, 128).transpose(3, 0, 1, 2).reshape(128, L * 32))
    lcv = np.stack([np.asarray(inputs["ln_cv_g"]), np.asarray(inputs["ln_cv_b"])], axis=1)
    m["lncv"] = f(np.broadcast_to(lcv.reshape(1, L * 512), (128, L * 512)))
    bs = np.asarray(inputs["b_sp"])
    bc = bs.reshape(L, 2, 2, 128)
    m["bspc"] = f(np.repeat(bc.transpose(2, 0, 1, 3), 64, axis=0).reshape(128, L * 256))
    ws = np.asarray(inputs["w_sp"])
    m["wspT"] = f(ws.transpose(3, 0, 1, 2).reshape(128, L * 512))
    half = 8
    inv = (np.float32(500000.0) ** (-np.arange(half, dtype=np.float32) / np.float32(half))).astype(np.float32)
    def cstab(pos):
        ang = pos.astype(np.float32)[:, None] * inv[None, :]
        return np.concatenate([np.cos(ang), np.sin(ang)], axis=1).astype(np.float32)
    m["csp"] = cstab(np.arange(S))
    m["css"] = cstab(NPG * 128 + np.arange(4))
    m["cident"] = np.eye(128, dtype=np.float32)
    ii = np.arange(128)
    m["ctriu"] = (ii[:, None] <= ii[None, :]).astype(np.float32)
    m["ctneg"] = np.where(ii[None, :] <= ii[:, None], 0.0, -1e30).astype(np.float32)
    m["ciota"] = ii.astype(np.float32).reshape(128, 1)
    m["ck"] = f(np.asarray(inputs["cache_k"]).reshape(L * NPOOL * 128, 128))
    m["cv"] = f(np.asarray(inputs["cache_v"]).reshape(L * NPOOL * 128, 128))
    m["cki"] = f(np.asarray(inputs["cache_kidx"]).reshape(L * NPOOL * 128, 64))
    m["pt"] = np.ascontiguousarray(np.asarray(inputs["page_table"])[core * NB:(core + 1) * NB].reshape(1, NB * NPG), dtype=np.int32)
    sc_ = np.asarray(inputs["state_conv"])[:, core * NB:(core + 1) * NB]
    m["stcT"] = f(sc_.transpose(0, 3, 1, 2).reshape(L, 256, NB * 2))
    sf_ = np.asarray(inputs["state_ffn_conv"])[:, core * NB:(core + 1) * NB]
    m["stfT"] = f(sf_.transpose(0, 3, 1, 2).reshape(L, 2 * DFF, NB * 2))
    return m


_CACHE = {}


def run(inputs, cfg, ncores=2):
    key = tuple(sorted(cfg.items()))
    if key not in _CACHE:
        _CACHE[key] = build(cfg)[0]
    nc = _CACHE[key]
    maps = [_host_layout(inputs, cfg, c) for c in range(ncores)]
    res = run_bass_kernel_spmd(nc, maps, core_ids=list(range(ncores))).results
    S, L, NB = cfg["S"], cfg["L"], cfg["NB"]
    cat = lambda k, ax: np.concatenate([np.asarray(r[k]) for r in res], axis=ax)
    st = lambda k: np.stack([np.asarray(r[k]) for r in res], axis=0)
    y_p = st("y_p")
    y_s = cat("y_s", 0).reshape(ncores * NB, 4, D)
    nk_p = st("nk_p").transpose(1, 0, 2, 3).reshape(L, ncores, S, 2, 64)
    nv_p = st("nv_p").transpose(1, 0, 2, 3).reshape(L, ncores, S, 2, 64)
    nki_p = st("nki_p").transpose(1, 0, 2, 3)
    nk_s = cat("nk_s", 1).reshape(L, ncores * NB, 4, 2, 64)
    nv_s = cat("nv_s", 1).reshape(L, ncores * NB, 4, 2, 64)
    nki_s = cat("nki_s", 1).reshape(L, ncores * NB, 4, 64)
    ncv_p = st("ncv_p").transpose(1, 0, 3, 2)
    ncv_s = cat("ncv_s", 2).reshape(L, 256, ncores * NB, 2).transpose(0, 2, 3, 1)
    nff_p = st("nff_p").transpose(1, 0, 3, 2)
    nff_s = cat("nff_s", 2).reshape(L, 2 * DFF, ncores * NB, 2).transpose(0, 2, 3, 1)
    nch_s = cat("nch_s", 1).reshape(L, ncores * NB, 4, 256)
    outs = (y_p, y_s, nk_p, nk_s, nv_p, nv_s, nki_p, nki_s, ncv_p, ncv_s, nff_p, nff_s, nch_s)
    return tuple(np.ascontiguousarray(o, dtype=np.float32) for o in outs)


def kernel(**inputs):
    xp = np.asarray(inputs["x_prompt"])
    ptab = np.asarray(inputs["page_table"])
    cfg = dict(S=xp.shape[1], L=np.asarray(inputs["w_in"]).shape[0], NB=ptab.shape[0] // 2,
               NPG=ptab.shape[1], NPOOL=np.asarray(inputs["cache_k"]).shape[1])
    return run(inputs, cfg, ncores=2)
```

```python
import sys
import numpy as np
from contextlib import ExitStack
import concourse.bass as bass
import concourse.mybir as mybir
from concourse.bass_utils import run_bass_kernel_spmd

F32 = mybir.dt.float32
BF16 = mybir.dt.bfloat16
I32 = mybir.dt.int32
AF = mybir.ActivationFunctionType
ALU = mybir.AluOpType

D = 1024
DFF = 2816
NCH_F = 44
ALPHA = 8.0 ** 0.25
LN_EPS = 1e-5
NIT = 22
KSEL = 256
NEG = -30000.0
WCOLS = 2696


class Buf:
    __slots__ = ("w", "r")

    def __init__(self):
        self.w = None
        self.r = {}


class Prog:
    CE = ("tensor", "vector", "scalar", "gpsimd")

    def __init__(self, nc, es, ndma=12):
        self.nc = nc
        self.q = {e: [] for e in ("tensor", "vector", "scalar", "gpsimd", "sync")}
        self.semh = {}
        self.cnt = {}
        self.waited = {e: {} for e in self.q}
        for e in self.CE:
            self.semh[e] = es.enter_context(nc.semaphore("s_" + e))
            self.cnt[e] = 0
        self.dsem = {}
        self.dcnt = {}
        self.dnext = {}
        for qn in ("sync", "gpsimd"):
            self.dsem[qn] = []
            for i in range(ndma):
                k = "d_%s_%d" % (qn, i)
                self.semh[k] = es.enter_context(nc.semaphore(k))
                self.dsem[qn].append(k)
                self.dcnt[k] = 0
            self.dnext[qn] = 0
        self.n = 0
        self.nops = 0

    def _waits(self, eng, reads, writes):
        need = {}
        for b in reads:
            if b.w is not None:
                s, v = b.w
                if need.get(s, 0) < v:
                    need[s] = v
        for b in writes:
            if b.w is not None:
                s, v = b.w
                if need.get(s, 0) < v:
                    need[s] = v
            for s, v in b.r.items():
                if need.get(s, 0) < v:
                    need[s] = v
        out = []
        wd = self.waited[eng]
        for s, v in need.items():
            if eng == "tensor" and s == "tensor":
                continue
            if wd.get(s, 0) >= v:
                continue
            wd[s] = v
            out.append((s, v))
        return out

    def _mark(self, ev, reads, writes):
        s, v = ev
        for b in reads:
            if b.r.get(s, 0) < v:
                b.r[s] = v
        for b in writes:
            b.w = ev
            b.r = {}

    LIMIT = None
    MARKS = []
    LOG = []

    def op(self, eng, reads, writes, fn):
        if Prog.LIMIT is not None and self.nops >= Prog.LIMIT:
            return
        self.nops += 1
        Prog.LOG.append((self.nops, eng, sys._getframe(2).f_lineno))
        waits = self._waits(eng, reads, writes)
        self.cnt[eng] += 1
        ev = (eng, self.cnt[eng])
        semh = self.semh
        sh = semh[eng]

        def run(e):
            for s, v in waits:
                e.wait_ge(semh[s], v)
            fn(e).then_inc(sh, 1)
        self.q[eng].append(run)
        self._mark(ev, reads, writes)
        self.n += 1 + len(waits)

    def dma(self, qn, reads, writes, fn):
        if Prog.LIMIT is not None and self.nops >= Prog.LIMIT:
            return
        self.nops += 1
        Prog.LOG.append((self.nops, 'dma-' + qn, sys._getframe(2).f_lineno))
        waits = self._waits(qn, reads, writes)
        i = self.dnext[qn]
        self.dnext[qn] = (i + 1) % len(self.dsem[qn])
        k = self.dsem[qn][i]
        prev = self.dcnt[k]
        if prev > 0 and self.waited[qn].get(k, 0) < 16 * prev:
            waits.append((k, 16 * prev))
            self.waited[qn][k] = 16 * prev
        self.dcnt[k] = prev + 1
        ev = (k, 16 * (prev + 1))
        semh = self.semh
        sh = semh[k]

        def run(e):
            for s, v in waits:
                e.wait_ge(semh[s], v)
            fn(e).then_inc(sh, 16)
        self.q[qn].append(run)
        self._mark(ev, reads, writes)
        self.n += 1 + len(waits)

    def finish(self):
        waits = [(k, 16 * c) for k, c in self.dcnt.items() if c > 0]
        semh = self.semh

        def run(e):
            for s, v in waits:
                e.wait_ge(semh[s], v)
        self.q["sync"].append(run)


def build(cfg):
    S, L, NB, NPG, NPOOL = cfg["S"], cfg["L"], cfg["NB"], cfg["NPG"], cfg["NPOOL"]
    PAST = NPG * 128
    NTP = 256
    NTS = NB * 4
    KMAX = max(S, PAST + 4)
    NKC = max(S // 128, NPG + 1)
    nc = bass.Bass("TRN2", target_bir_lowering=False)

    def din(name, shape, dt=F32):
        return nc.dram_tensor(name, list(shape), dt, kind="ExternalInput")

    def dout(name, shape, dt=F32):
        return nc.dram_tensor(name, list(shape), dt, kind="ExternalOutput")

    xT = din("xT", [D, S])
    xsT = din("xsT", [D, NTS])
    w_in = din("w_in", [L, D, WCOLS])
    w_o = din("w_o", [L, D, D])
    w_up = din("w_up", [L, D, 2 * DFF])
    w_dn = din("w_dn", [L, DFF, D])
    cva = din("cva", [128, L * 2 * 3])
    cvf = din("cvf", [128, L * NCH_F * 3])
    lng = din("lng", [128, L * 4 * 8])
    lncv = din("lncv", [128, L * 2 * 256])
    bspc = din("bspc", [128, L * 2 * 128])
    wspT = din("wspT", [128, L * 4 * 128])
    csp = din("csp", [S, 16])
    css = din("css", [4, 16])
    cident = din("cident", [128, 128])
    ctriu = din("ctriu", [128, 128])
    ctneg = din("ctneg", [128, 128])
    ciota = din("ciota", [128, 1])
    ck = din("ck", [L * NPOOL * 128, 128])
    cv = din("cv", [L * NPOOL * 128, 128])
    cki = din("cki", [L * NPOOL * 128, 64])
    pt = din("pt", [1, NB * NPG], I32)
    stcT = din("stcT", [L, 256, NB * 2])
    stfT = din("stfT", [L, 2 * DFF, NB * 2])

    y_p = dout("y_p", [S, D])
    y_s = dout("y_s", [NTS, D])
    nk_p = dout("nk_p", [L, S, 128])
    nk_s = dout("nk_s", [L, NTS, 128])
    nv_p = dout("nv_p", [L, S, 128])
    nv_s = dout("nv_s", [L, NTS, 128])
    nki_p = dout("nki_p", [L, S, 64])
    nki_s = dout("nki_s", [L, NTS, 64])
    ncv_p = dout("ncv_p", [L, 256, 2])
    ncv_s = dout("ncv_s", [L, 256, NB * 2])
    nff_p = dout("nff_p", [L, 2 * DFF, 2])
    nff_s = dout("nff_s", [L, 2 * DFF, NB * 2])
    nch_s = dout("nch_s", [L, NTS, 256])
    hsc_p = [nc.dram_tensor("hsc_p%d" % i, [D, S], F32, kind="Internal") for i in range(2)]
    hsc_s = [nc.dram_tensor("hsc_s%d" % i, [D, NTS], F32, kind="Internal") for i in range(2)]
    wis = nc.dram_tensor("wis", [L, 11, 128, 8, 256], BF16, kind="Internal")
    wos = nc.dram_tensor("wos", [L, 8, 128, 12, 128], BF16, kind="Internal")
    wus = nc.dram_tensor("wus", [L, 22, 128, 8, 2, 128], BF16, kind="Internal")
    wds = nc.dram_tensor("wds", [L, 8, 2, 128, 11, 128], BF16, kind="Internal")

    es = ExitStack()
    with es:
        P = Prog(nc, es)

        def sb(name, shape, dt=F32):
            return es.enter_context(nc.sbuf_tensor(name, list(shape), dt))

        ident = sb("ident", [128, 128]); identb = sb("identb", [128, 128], BF16)
        identrep = sb("identrep", [128, 4, 128], BF16)
        onesf = sb("onesf", [128, 128])
        triu = sb("triu", [128, 128]); tneg = sb("tneg", [128, 128])
        iota = sb("iota", [128, 1])
        cva_t = sb("cva_t", [128, L * 6]); cvf_t = sb("cvf_t", [128, L * NCH_F * 3])
        lng_t = sb("lng_t", [128, L * 32]); lncv_t = sb("lncv_t", [128, 512])
        bspc_t = sb("bspc_t", [128, 256])
        wmT = sb("wmT", [128, 4, 128], BF16)
        wtmp = sb("wtmp", [128, 128])
        idx = sb("idx", [128, NPG], I32)
        iol = sb("iol", [128, L])
        ptb = sb("ptb", [128, NPG], I32)
        kT = sb("kT", [128, KMAX], BF16)
        kiT = sb("kiT", [128, KMAX], BF16)
        Vp = sb("Vp", [128, NKC, 2, 65], BF16)
        sc = sb("sc", [128, max(KMAX, 4224)])
        junk = sb("junk", [128, 2048], mybir.dt.uint8)
        xt = sb("xt", [128, 8, NTP]); xb = sb("xb", [128, 8, NTP], BF16)
        fm = sb("fm", [128, 8, NTP])
        zt = sb("zt", [128, 2, NTP + 2 * NB])
        hl = sb("hl", [128, NCH_F, 2])
        stfc = [sb("stfc%d" % i, [128, NB * 2]) for i in range(4)]
        aT = sb("aT", [128, 2, NTP], BF16); cT = sb("cT", [128, 2, NTP], BF16)
        oT = sb("oT", [64, 2, 4, NTP], BF16)
        qT = sb("qT", [128, 4, 128], BF16); qiT = sb("qiT", [128, 4, 128], BF16)
        qq = sb("qq", [128, 1024]); qqb = sb("qqb", [128, 1024], BF16)
        kv = sb("kv", [128, 392]); kvb = sb("kvb", [128, 392], BF16)
        rt = sb("rt", [128, 4, 16, 8])
        cs = sb("cs", [128, 16])
        sm = sb("sm", [128, 64])
        Dm = sb("Dm", [128, 8, 128], BF16)
        vn = sb("vn", [128, 256]); vnb = sb("vnb", [128, 256], BF16)
        bst = sb("bst", [128, 8])
        rh = [sb("rh%d" % i, [128, 512], BF16) for i in range(4)]
        pT = [sb("pT%d" % i, [128, 512], BF16) for i in range(4)]
        nmc = [sb("nmc%d" % i, [128, 128], BF16) for i in range(4)]
        srow = sb("srow", [128, 512]); rs = sb("rs", [64, 512])
        winb = [sb("winb%d" % i, [128, 8, 256], BF16) for i in range(3)]
        wob = [sb("wob%d" % i, [128, 12, 128], BF16) for i in range(2)]
        wupb = [sb("wupb%d" % i, [128, 8, 2, 128], BF16) for i in range(2)]
        wdnb = [sb("wdnb%d" % i, [128, 11, 128], BF16) for i in range(3)]
        hs = [sb("hs%d" % i, [128, NTP + 2 * NB]) for i in range(4)]
        yv = [sb("yv%d" % i, [128, NTP]) for i in range(4)]
        mt = sb("mt", [128, NTP]); rt2 = sb("rt2", [128, NTP]); msq = sb("msq", [128, NTP])
        ctmp = sb("ctmp", [128, 128])
        kpg = [sb("kpg%d" % i, [128, 128]) for i in range(2)]
        vpg = [sb("vpg%d" % i, [128, 128]) for i in range(2)]
        ipg = [sb("ipg%d" % i, [128, 64]) for i in range(2)]
        kpb = [sb("kpb%d" % i, [128, 128], BF16) for i in range(2)]
        ipb = [sb("ipb%d" % i, [128, 128], BF16) for i in range(2)]
        pbank = [es.enter_context(nc.psum_tensor("pb%d" % i, [128, 512], F32)) for i in range(6)]
        pbank.append(es.enter_context(nc.psum_tensor("pb6", [128, 8, 128], BF16)))
        pbank.append(es.enter_context(nc.psum_tensor("pb7", [128, 8, 128], BF16)))
        B = {}

        def buf(o):
            if isinstance(o, Buf):
                return o
            k = o if isinstance(o, (tuple, str)) else id(o)
            if k not in B:
                B[k] = Buf()
            return B[k]
        Rn = [0]

        def R():
            i = Rn[0] % 4
            Rn[0] += 1
            return pbank[i]
        ptb_bf = pbank[7]
        pt2_bf = pbank[6]

        psum_ids = set(id(x) for x in pbank)

        def _rw(r, w):
            rr = [buf(x) for x in r if id(x) not in psum_ids]
            ww = [buf(x) for x in w] + [buf(x) for x in r if id(x) in psum_ids]
            return rr, ww
        def V(r, w, fn): P.op("vector", *_rw(r, w), fn)
        def A(r, w, fn): P.op("scalar", *_rw(r, w), fn)
        def G(r, w, fn): P.op("gpsimd", *_rw(r, w), fn)
        def T(r, w, fn): P.op("tensor", *_rw(r, w), fn)
        def DS(r, w, fn): P.dma("sync", [buf(x) for x in r], [buf(x) for x in w], fn)
        def DG(r, w, fn): P.dma("gpsimd", [buf(x) for x in r], [buf(x) for x in w], fn)

        def dma_in(q, dst_t, dst_ap, src_ap, rd=()):
            (DS if q == "s" else DG)(list(rd), [dst_t], lambda e: e.dma_start(out=dst_ap, in_=src_ap))

        def dma_out(q, dst_ap, src_t, src_ap, wr=()):
            (DS if q == "s" else DG)([src_t], list(wr), lambda e: e.dma_start(out=dst_ap, in_=src_ap))

        dma_in("s", ident, ident[:, :], cident[:, :])
        dma_in("s", triu, triu[:, :], ctriu[:, :])
        dma_in("s", tneg, tneg[:, :], ctneg[:, :])
        dma_in("s", iota, iota[:, :], ciota[:, :])
        dma_in("s", cva_t, cva_t[:, :], cva[:, :])
        dma_in("s", cvf_t, cvf_t[:, :], cvf[:, :])
        dma_in("s", lng_t, lng_t[:, :], lng[:, :])
        V([ident], [identb], lambda e: e.tensor_copy(out=identb[:, :], in_=ident[:, :]))
        for r in range(4):
            V([ident], [identrep], lambda e, r=r: e.tensor_copy(out=identrep[:, r, :], in_=ident[:, :]))
        V([], [onesf], lambda e: e.memset(onesf[:, :], 1.0))
        V([], [Vp], lambda e: e.memset(Vp[:, :, :, 64:65], 1.0))
        for l_ in range(L):
            V([iota], [iol], lambda e, l_=l_: e.tensor_scalar(out=iol[:, l_:l_ + 1], in0=iota[:, 0:1], scalar1=float(l_ * NPOOL * 128),
                                                              scalar2=None, op0=ALU.add))

        wi_n = [0]
        wd_n = [0]

        def mm_group(out_t, out_ap, pairs, rd):
            n = len(pairs)
            for i, (l_, r_) in enumerate(pairs):
                T(rd, [out_t],
                  lambda e, l_=l_, r_=r_, i=i: e.matmul(out_ap, l_, r_, start=(i == 0), stop=(i == n - 1)))

        def process_sb(l, kind, sbi):
            Prog.MARKS.append(('sb', l, kind, sbi, P.nops))
            prm = kind == "p"
            NT = NTP if prm else NTS
            tok0 = sbi * NTP if prm else 0
            nq = 128 if prm else 4
            ntm = NT // nq
            if l == 0:
                src = xT if prm else xsT
            else:
                src = (hsc_p if prm else hsc_s)[(l - 1) % 2]
            srcb = ("h", prm, (l - 1) % 2, sbi)
            dma_in("s", xt, xt[:, :, 0:NT], src[:, tok0:tok0 + NT].rearrange("(c p) t -> p c t", p=128),
                   rd=[srcb] if l > 0 else [])
            A([xt], [xb], lambda e: e.activation(out=xb[:, :, 0:NT], in_=xt[:, :, 0:NT], func=AF.Copy))
            for half in range(4):
                wb = winb[wi_n[0] % 3]; wi_n[0] += 1
                dma_in("s", wb, wb[:, :, :], wis[l, half, :, :, :], rd=[("w", "in", l)])
                for j in range(2):
                    oc = half * 2 + j
                    ps = R()
                    mm_group(ps, ps[:, 0:NT], [(wb[:, k, j * 128:(j + 1) * 128], xb[:, k, 0:NT]) for k in range(8)], [wb, xb])
                    A([ps], [fm], lambda e, ps=ps, oc=oc: e.activation(out=fm[:, oc, 0:NT], in_=ps[:, 0:NT], func=AF.Copy))
            if prm:
                zv = zt[:, :, 0:NT + 2].rearrange("p c (b t) -> p c b t", b=1)
                if sbi == 0:
                    V([], [zt], lambda e: e.memset(zt[:, :, 0:2], 0.0))
                else:
                    A([zt], [zt], lambda e: e.activation(out=zt[:, :, 0:2], in_=zt[:, :, NT:NT + 2], func=AF.Copy))
                nbk, tt = 1, NT
            else:
                zv = zt[:, :, 0:NB * 6].rearrange("p c (b t) -> p c b t", b=NB)
                for c_ in range(2):
                    DS([], [buf(zt)], lambda e, c_=c_: e.dma_start(
                        out=zv[:, c_, :, 0:2], in_=stcT[l, c_ * 128:(c_ + 1) * 128, :].rearrange("p (b i) -> p b i", i=2)))
                nbk, tt = NB, 4
            fmv = fm[:, :, 0:NT].rearrange("p c (b t) -> p c b t", b=nbk)
            V([fm, zt], [zt], lambda e: e.tensor_tensor(out=zv[:, :, :, 2:2 + tt], in0=fmv[:, 2:4, :, :],
                                                         in1=fmv[:, 4:6, :, :], op=ALU.mult))
            if prm and sbi == S // NTP - 1:
                dma_out("s", ncv_p[l, :, :].rearrange("(c p) i -> p c i", p=128), zt, zt[:, :, NT:NT + 2])
            if not prm:
                for c_ in range(2):
                    dma_out("s", ncv_s[l, c_ * 128:(c_ + 1) * 128, :].rearrange("p (b i) -> p b i", i=2), zt, zv[:, c_, :, 4:6])
            for c in range(2):
                yy = yv[c]
                yyv = yy[:, 0:NT].rearrange("p (b t) -> p b t", b=nbk)
                w = lambda i, c=c: cva_t[:, (l * 2 + c) * 3 + i:(l * 2 + c) * 3 + i + 1]
                V([zt, cva_t], [yy], lambda e, c=c, yyv=yyv, w=w: e.tensor_scalar(
                    out=yyv, in0=zv[:, c, :, 2:2 + tt], scalar1=w(2), scalar2=None, op0=ALU.mult))
                V([zt, yy], [yy], lambda e, c=c, yyv=yyv, w=w: e.scalar_tensor_tensor(
                    out=yyv, in0=zv[:, c, :, 1:1 + tt], scalar=w(1), in1=yyv, op0=ALU.mult, op1=ALU.add))
                V([zt, yy], [yy], lambda e, c=c, yyv=yyv, w=w: e.scalar_tensor_tensor(
                    out=yyv, in0=zv[:, c, :, 0:tt], scalar=w(0), in1=yyv, op0=ALU.mult, op1=ALU.add))
                V([yy, fm], [aT], lambda e, c=c, yy=yy: e.tensor_tensor(
                    out=aT[:, c, 0:NT], in0=yy[:, 0:NT], in1=fm[:, c, 0:NT], op=ALU.mult))
            for tm in range(ntm):
                t0 = tm * nq
                g0 = tok0 + t0
                if prm:
                    jblk = g0 // 128
                    koff = jblk * 128
                    vchunk = jblk
                    nkeys = koff + 128
                else:
                    koff = PAST
                    vchunk = NPG
                    nkeys = PAST + 4
                    Prog.MARKS.append(('pages', l, tm, P.nops))
                    stage_pages(l, tm)
                tm_block(l, prm, NT, nq, t0, g0, koff, vchunk, nkeys)
            for oc in range(8):
                wb = wob[oc % 2]
                dma_in("s", wb, wb[:, :, :], wos[l, oc, :, :, :], rd=[("w", "o", l)])
                pairs = [(wb[:, c, :], aT[:, c, 0:NT]) for c in range(2)]
                pairs += [(wb[0:64, 2 + h, :], oT[0:64, h // 4, h % 4, 0:NT]) for h in range(8)]
                pairs += [(wb[:, 10 + c, :], cT[:, c, 0:NT]) for c in range(2)]
                ps = R()
                mm_group(ps, ps[:, 0:NT], pairs, [wb, aT, oT, cT])
                V([ps, xt], [xt], lambda e, ps=ps, oc=oc: e.scalar_tensor_tensor(
                    out=xt[:, oc, 0:NT], in0=xt[:, oc, 0:NT], scalar=ALPHA, in1=ps[:, 0:NT], op0=ALU.mult, op1=ALU.add))
            layer_norm(l, 0, NT)
            A([xt], [xb], lambda e: e.activation(out=xb[:, :, 0:NT], in_=xt[:, :, 0:NT], func=AF.Copy))
            actT = sc[:, 0:11 * NTP].bitcast(BF16)
            hview = lambda t_: t_[:, 0:(NT + 2 * nbk)].rearrange("p (b t) -> p b t", b=nbk)
            for i in range(22):
                wb = wupb[i % 2]
                dma_in("s", wb, wb[:, :, :, :], wus[l, i, :, :, :, :], rd=[("w", "up", l)])
                ys = []
                for gu in range(2):
                    ch = gu * 22 + i
                    ps = R()
                    mm_group(ps, ps[:, 0:NT], [(wb[:, k, gu, :], xb[:, k, 0:NT]) for k in range(8)], [wb, xb])
                    h_ = hs[(2 * i + gu) % 4]
                    hv = hview(h_)
                    if prm:
                        if sbi == 0:
                            V([], [h_], lambda e, hv=hv: e.memset(hv[:, :, 0:2], 0.0))
                        else:
                            A([hl], [h_], lambda e, hv=hv, ch=ch: e.activation(out=hv[:, 0, 0:2], in_=hl[:, ch, :], func=AF.Copy))
                    else:
                        st_ = stfc[(2 * i + gu) % 4]
                        dma_in("s", st_, st_[:, :], stfT[l, ch * 128:(ch + 1) * 128, :])
                        A([st_], [h_], lambda e, hv=hv, st_=st_: e.activation(
                            out=hv[:, :, 0:2], in_=st_[:, :].rearrange("p (b i) -> p b i", i=2), func=AF.Copy))
                    A([ps], [h_], lambda e, hv=hv, ps=ps: e.activation(
                        out=hv[:, :, 2:2 + tt], in_=ps[:, 0:NT].rearrange("p (b t) -> p b t", b=nbk), func=AF.Copy))
                    if prm:
                        A([h_], [hl], lambda e, hv=hv, ch=ch: e.activation(out=hl[:, ch, :], in_=hv[:, 0, NT:NT + 2], func=AF.Copy))
                    else:
                        dma_out("s", nff_s[l, ch * 128:(ch + 1) * 128, :].rearrange("p (b i) -> p b i", i=2), h_, hv[:, :, 4:6])
                    yy = yv[(2 * i + gu) % 4]
                    yyv = yy[:, 0:NT].rearrange("p (b t) -> p b t", b=nbk)
                    w = lambda k_, ch=ch: cvf_t[:, (l * NCH_F + ch) * 3 + k_:(l * NCH_F + ch) * 3 + k_ + 1]
                    V([ps, cvf_t], [yy], lambda e, ps=ps, yyv=yyv, w=w: e.tensor_scalar(
                        out=yyv, in0=ps[:, 0:NT].rearrange("p (b t) -> p b t", b=nbk), scalar1=w(2), scalar2=None, op0=ALU.mult))
                    V([h_, yy], [yy], lambda e, hv=hv, yyv=yyv, w=w: e.scalar_tensor_tensor(
                        out=yyv, in0=hv[:, :, 1:1 + tt], scalar=w(1), in1=yyv, op0=ALU.mult, op1=ALU.add))
                    V([h_, yy], [yy], lambda e, hv=hv, yyv=yyv, w=w: e.scalar_tensor_tensor(
                        out=yyv, in0=hv[:, :, 0:tt], scalar=w(0), in1=yyv, op0=ALU.mult, op1=ALU.add))
                    ys.append(yy)
                A([ys[0]], [ys[0]], lambda e, y0=ys[0]: e.activation(out=y0[:, 0:NT], in_=y0[:, 0:NT], func=AF.Silu))
                V([ys[0], ys[1]], [sc], lambda e, i=i, y0=ys[0], y1=ys[1]: e.tensor_tensor(
                    out=actT[:, i * NT:(i + 1) * NT], in0=y0[:, 0:NT], in1=y1[:, 0:NT], op=ALU.mult))
            if prm and sbi == S // NTP - 1:
                dma_out("s", nff_p[l, :, :].rearrange("(c p) i -> p c i", p=128), hl, hl[:, :, :])
            for oc in range(8):
                ps = R()
                for hf in range(2):
                    wb = wdnb[wd_n[0] % 3]; wd_n[0] += 1
                    dma_in("s", wb, wb[:, :, :], wds[l, oc, hf, :, :, :], rd=[("w", "dn", l)])
                    for i in range(11):
                        ii = hf * 11 + i
                        T([wb, sc], [ps], lambda e, ps=ps, wb=wb, i=i, ii=ii: e.matmul(
                            ps[:, 0:NT], wb[:, i, :], actT[:, ii * NT:(ii + 1) * NT], start=(ii == 0), stop=(ii == 21)))
                V([ps, xt], [xt], lambda e, ps=ps, oc=oc: e.scalar_tensor_tensor(
                    out=xt[:, oc, 0:NT], in0=xt[:, oc, 0:NT], scalar=ALPHA, in1=ps[:, 0:NT], op0=ALU.mult, op1=ALU.add))
            layer_norm(l, 1, NT)
            if l < L - 1:
                dst = (hsc_p if prm else hsc_s)[l % 2]
                dma_out("s", dst[:, tok0:tok0 + NT].rearrange("(c p) t -> p c t", p=128), xt, xt[:, :, 0:NT],
                        wr=[("h", prm, l % 2, sbi)])
            else:
                nb_ = max(1, NT // 128)
                tw = min(NT, 128)
                for b_ in range(nb_):
                    for c in range(8):
                        ps = R()
                        T([xt, ident], [ps], lambda e, ps=ps, c=c, b_=b_: e.transpose(
                            ps[0:tw, 0:128], xt[:, c, b_ * tw:(b_ + 1) * tw], ident[:, :]))
                        A([ps], [sc], lambda e, ps=ps, c=c: e.activation(
                            out=sc[0:tw, c * 128:(c + 1) * 128], in_=ps[0:tw, 0:128], func=AF.Copy))
                    ydst = y_p if prm else y_s
                    dma_out("s", ydst[tok0 + b_ * tw:tok0 + (b_ + 1) * tw, :], sc, sc[0:tw, 0:1024])

        def layer_norm(l, which, NT):
            sq = sc[:, 0:8 * NTP].rearrange("p (c t) -> p c t", c=8)
            A([xt], [sc], lambda e: e.activation(out=sq[:, :, 0:NT], in_=xt[:, :, 0:NT], func=AF.Square))
            p1 = R(); p2 = R()
            mm_group(p1, p1[:, 0:NT], [(onesf[:, :], xt[:, c, 0:NT]) for c in range(8)], [onesf, xt])
            mm_group(p2, p2[:, 0:NT], [(onesf[:, :], sq[:, c, 0:NT]) for c in range(8)], [onesf, sc])
            A([p1], [mt], lambda e: e.mul(out=mt[:, 0:NT], in_=p1[:, 0:NT], mul=1.0 / D))
            V([mt], [msq], lambda e: e.tensor_tensor(out=msq[:, 0:NT], in0=mt[:, 0:NT], in1=mt[:, 0:NT], op=ALU.mult))
            V([p2, msq], [rt2], lambda e: e.scalar_tensor_tensor(
                out=rt2[:, 0:NT], in0=p2[:, 0:NT], scalar=1.0 / D, in1=msq[:, 0:NT], op0=ALU.mult, op1=ALU.subtract))
            V([rt2], [rt2], lambda e: e.tensor_scalar(out=rt2[:, 0:NT], in0=rt2[:, 0:NT], scalar1=LN_EPS, scalar2=None, op0=ALU.add))
            A([rt2], [rt2], lambda e: e.activation(out=rt2[:, 0:NT], in_=rt2[:, 0:NT], func=AF.Sqrt))
            V([rt2], [rt2], lambda e: e.reciprocal(out=rt2[:, 0:NT], in_=rt2[:, 0:NT]))
            V([xt, mt], [xt], lambda e: e.tensor_tensor(
                out=xt[:, :, 0:NT], in0=xt[:, :, 0:NT], in1=mt[:, 0:NT].unsqueeze(1).to_broadcast([128, 8, NT]), op=ALU.subtract))
            V([xt, rt2], [xt], lambda e: e.tensor_tensor(
                out=xt[:, :, 0:NT], in0=xt[:, :, 0:NT], in1=rt2[:, 0:NT].unsqueeze(1).to_broadcast([128, 8, NT]), op=ALU.mult))
            for c in range(8):
                gi = l * 32 + (2 * which) * 8 + c
                bi = l * 32 + (2 * which + 1) * 8 + c
                eng = V
                eng([xt, lng_t], [xt], lambda e, c=c, gi=gi, bi=bi: e.tensor_scalar(
                    out=xt[:, c, 0:NT], in0=xt[:, c, 0:NT], scalar1=lng_t[:, gi:gi + 1], scalar2=lng_t[:, bi:bi + 1],
                    op0=ALU.mult, op1=ALU.add))

        def rotary(tile_t, ap3, H, nq):
            x1 = ap3[:, :, 0:8]; x2 = ap3[:, :, 8:16]
            cb = cs[0:nq, 0:8].unsqueeze(1).to_broadcast([nq, H, 8])
            sb_ = cs[0:nq, 8:16].unsqueeze(1).to_broadcast([nq, H, 8])
            t = [rt[0:nq, i, 0:H, :] for i in range(4)]
            V([tile_t, cs], [rt], lambda e: e.tensor_tensor(out=t[0], in0=x1, in1=cb, op=ALU.mult))
            V([tile_t, cs], [rt], lambda e: e.tensor_tensor(out=t[1], in0=x2, in1=sb_, op=ALU.mult))
            V([tile_t, cs], [rt], lambda e: e.tensor_tensor(out=t[2], in0=x2, in1=cb, op=ALU.mult))
            V([tile_t, cs], [rt], lambda e: e.tensor_tensor(out=t[3], in0=x1, in1=sb_, op=ALU.mult))
            V([rt], [tile_t], lambda e: e.tensor_tensor(out=x1, in0=t[0], in1=t[1], op=ALU.subtract))
            V([rt], [tile_t], lambda e: e.tensor_tensor(out=x2, in0=t[2], in1=t[3], op=ALU.add))

        pg_n = [0]

        def stage_pages(l, b):
            dma_in("s", ptb, ptb[:, :], pt[0:1, b * NPG:(b + 1) * NPG].to_broadcast([128, NPG]))
            V([ptb, iol], [idx], lambda e: e.tensor_scalar(out=idx[:, :], in0=ptb[:, :], scalar1=128.0,
                                                           scalar2=iol[:, l:l + 1], op0=ALU.mult, op1=ALU.add))
            for j in range(NPG):
                s_ = pg_n[0] % 2; pg_n[0] += 1
                ia = idx[:, j:j + 1]
                for (dst, srcT) in ((kpg[s_], ck), (vpg[s_], cv), (ipg[s_], cki)):
                    DG([idx], [buf(dst)], lambda e, dst=dst, srcT=srcT, ia=ia: e.indirect_dma_start(
                        out=dst[:, :], out_offset=None, in_=srcT[:, :],
                        in_offset=bass.IndirectOffsetOnAxis(ap=ia, axis=0)))
                V([kpg[s_]], [kpb[s_]], lambda e, s_=s_: e.tensor_copy(out=kpb[s_][:, :], in_=kpg[s_][:, :]))
                V([ipg[s_]], [ipb[s_]], lambda e, s_=s_: e.tensor_copy(
                    out=ipb[s_][:, :].rearrange("p (a d) -> p a d", a=2), in_=ipg[s_][:, :].unsqueeze(1).to_broadcast([128, 2, 64])))
                A([vpg[s_]], [Vp], lambda e, s_=s_, j=j: e.activation(
                    out=Vp[:, j, :, 0:64], in_=vpg[s_][:, :].rearrange("p (g d) -> p g d", g=2), func=AF.Copy))
                T([kpb[s_], identb], [pbank[7]], lambda e, s_=s_: e.transpose(ptb_bf[:, 0, :], kpb[s_][:, :], identb[:, :]))
                T([ipb[s_], identb], [pbank[7]], lambda e, s_=s_: e.transpose(ptb_bf[:, 1, :], ipb[s_][:, :], identb[:, :]))
                A([pbank[7]], [kT], lambda e, j=j: e.activation(out=kT[:, j * 128:(j + 1) * 128], in_=ptb_bf[:, 0, :], func=AF.Copy))
                A([pbank[7]], [kiT], lambda e, j=j: e.activation(out=kiT[:, j * 128:(j + 1) * 128], in_=ptb_bf[:, 1, :], func=AF.Copy))

        def tm_block(l, prm, NT, nq, t0, g0, koff, vchunk, nkeys):
            Prog.MARKS.append(('tm', l, prm, g0, P.nops))
            if prm:
                dma_in("s", cs, cs[0:nq, :], csp[g0:g0 + nq, :])
            else:
                dma_in("s", cs, cs[0:nq, :], css[0:nq, :])
            xcols = lambda k: xb[:, k, t0:t0 + nq]
            groups = [(1024, 512), (1536, 512), (2048, 392), (2440, 256)]
            pss = []
            pj = 4
            for gi, (c0, cw) in enumerate(groups):
                ps = R()
                for p0 in range(0, cw, 256):
                    pw = min(256, cw - p0)
                    wb = winb[wi_n[0] % 3]; wi_n[0] += 1
                    dma_in("s", wb, wb[:, :, 0:pw], wis[l, pj, :, :, 0:pw], rd=[("w", "in", l)])
                    pj += 1
                    mm_group(ps, ps[0:nq, p0:p0 + pw], [(xcols(k), wb[:, k, 0:pw]) for k in range(8)], [wb, xb])
                if gi == 0:
                    A([ps], [qq], lambda e, ps=ps: e.activation(out=qq[0:nq, 0:512], in_=ps[0:nq, 0:512], func=AF.Copy))
                elif gi == 1:
                    A([ps], [qq], lambda e, ps=ps: e.activation(out=qq[0:nq, 512:1024], in_=ps[0:nq, 0:512], func=AF.Copy))
                elif gi == 2:
                    A([ps], [kv], lambda e, ps=ps: e.activation(out=kv[0:nq, :], in_=ps[0:nq, 0:392], func=AF.Copy))
                else:
                    psvc = ps
            V([psvc], [bst], lambda e: e.bn_stats(out=bst[0:nq, 0:6], in_=psvc[0:nq, 0:256]))
            V([bst], [sm], lambda e: e.bn_aggr(out=sm[0:nq, 0:2], in_=bst[0:nq, 0:6]))
            V([sm], [sm], lambda e: e.tensor_scalar(out=sm[0:nq, 2:3], in0=sm[0:nq, 1:2], scalar1=LN_EPS, scalar2=None, op0=ALU.add))
            A([sm], [sm], lambda e: e.activation(out=sm[0:nq, 3:4], in_=sm[0:nq, 2:3], func=AF.Sqrt))
            V([sm], [sm], lambda e: e.reciprocal(out=sm[0:nq, 4:5], in_=sm[0:nq, 3:4]))
            V([psvc, sm], [vn], lambda e: e.tensor_scalar(out=vn[0:nq, :], in0=psvc[0:nq, 0:256], scalar1=sm[0:nq, 0:1],
                                                          scalar2=sm[0:nq, 4:5], op0=ALU.subtract, op1=ALU.mult))
            V([vn, lncv_t], [vn], lambda e: e.tensor_tensor(out=vn[0:nq, :], in0=vn[0:nq, :],
                                                            in1=lncv_t[0:nq, 0:256], op=ALU.mult))
            V([vn, lncv_t], [vn], lambda e: e.tensor_tensor(out=vn[0:nq, :], in0=vn[0:nq, :],
                                                            in1=lncv_t[0:nq, 256:512], op=ALU.add))
            A([vn], [vnb], lambda e: e.activation(out=vnb[0:nq, :], in_=vn[0:nq, :], func=AF.Copy))
            if not prm:
                dma_out("s", nch_s[l, g0:g0 + nq, :], vn, vn[0:nq, :])
            rotary(qq, qq[0:nq, :].rearrange("p (h d) -> p h d", d=64), 16, nq)
            rotary(kv, kv[0:nq, 0:128].rearrange("p (h d) -> p h d", d=64), 2, nq)
            rotary(kv, kv[0:nq, 256:384].rearrange("p (h d) -> p h d", d=64), 2, nq)
            ok, ov, oki = (nk_p, nv_p, nki_p) if prm else (nk_s, nv_s, nki_s)
            dma_out("s", ok[l, g0:g0 + nq, :], kv, kv[0:nq, 0:128])
            dma_out("s", ov[l, g0:g0 + nq, :], kv, kv[0:nq, 128:256])
            dma_out("s", oki[l, g0:g0 + nq, :], kv, kv[0:nq, 256:320])
            V([kv], [sm], lambda e: e.tensor_scalar(out=sm[0:nq, 8:16], in0=kv[0:nq, 384:392], scalar1=0.0, scalar2=2.0,
                                                    op0=ALU.is_gt, op1=ALU.mult))
            V([sm], [sm], lambda e: e.tensor_scalar(out=sm[0:nq, 8:16], in0=sm[0:nq, 8:16], scalar1=-1.0, scalar2=None, op0=ALU.add))
            V([kv, sm], [sm], lambda e: e.scalar_tensor_tensor(out=sm[0:nq, 16:24], in0=kv[0:nq, 384:392],
                                                               scalar=float(8.0 ** -0.5 * 0.125), in1=sm[0:nq, 8:16],
                                                               op0=ALU.mult, op1=ALU.mult))
            V([qq, sm], [qqb], lambda e: e.tensor_tensor(
                out=qqb[0:nq, 512:1024].rearrange("p (h d) -> p h d", d=64),
                in0=qq[0:nq, 512:1024].rearrange("p (h d) -> p h d", d=64),
                in1=sm[0:nq, 16:24].unsqueeze(2).to_broadcast([nq, 8, 64]), op=ALU.mult))
            A([qq], [qqb], lambda e: e.activation(out=qqb[0:nq, 0:512], in_=qq[0:nq, 0:512], func=AF.Copy))
            A([kv], [kvb], lambda e: e.activation(out=kvb[0:nq, :], in_=kv[0:nq, :], func=AF.Copy))
            V([kv], [Vp], lambda e: e.tensor_copy(out=Vp[0:nq, vchunk, :, 0:64],
                                                  in_=kv[0:nq, 128:256].rearrange("p (g d) -> p g d", g=2)))
            for h in range(8):
                V([ident, sm], [Dm], lambda e, h=h: e.tensor_scalar(
                    out=Dm[0:nq, h, 0:nq], in0=ident[0:nq, 0:nq], scalar1=sm[0:nq, 8 + h:9 + h], scalar2=None, op0=ALU.mult))
            for r in range(8):
                T([qqb, identb], [pbank[7]], lambda e, r=r: e.transpose(
                    ptb_bf[:, r, 0:nq], qqb[0:nq, r * 128:(r + 1) * 128], identb[0:nq, 0:nq]))
            A([pbank[7]], [qT], lambda e: e.activation(out=qT[:, :, 0:nq], in_=ptb_bf[:, 0:4, 0:nq], func=AF.Copy))
            A([pbank[7]], [qiT], lambda e: e.activation(out=qiT[:, :, 0:nq], in_=ptb_bf[:, 4:8, 0:nq], func=AF.Copy))
            T([kvb, identb], [pbank[6]], lambda e: e.transpose(pt2_bf[:, 0, 0:nq], kvb[0:nq, 0:128], identb[0:nq, 0:nq]))
            T([kvb, identb], [pbank[6]], lambda e: e.transpose(pt2_bf[:, 1, 0:nq], kvb[0:nq, 256:384], identb[0:nq, 0:nq]))
            A([pbank[6]], [kT], lambda e: e.activation(out=kT[:, koff:koff + nq], in_=pt2_bf[:, 0, 0:nq], func=AF.Copy))
            A([pbank[6]], [kiT], lambda e: e.activation(out=kiT[:, koff:koff + nq], in_=pt2_bf[:, 1, 0:nq], func=AF.Copy))
            cps = R()
            for c in range(2):
                for hh in range(2):
                    h = 2 * c + hh
                    T([vnb, wmT], [cps], lambda e, c=c, hh=hh, h=h: e.matmul(
                        cps[64 * hh:64 * hh + 64, c * nq:(c + 1) * nq], vnb[0:nq, h * 64:(h + 1) * 64], wmT[0:nq, h, 0:nq],
                        start=True, stop=True))
            for c in range(2):
                V([cps, bspc_t], [ctmp], lambda e, c=c: e.tensor_tensor(
                    out=ctmp[:, 0:nq], in0=cps[:, c * nq:(c + 1) * nq],
                    in1=bspc_t[:, c * 128:c * 128 + nq], op=ALU.add))
                V([ctmp, fm], [cT], lambda e, c=c: e.tensor_tensor(
                    out=cT[:, c, t0:t0 + nq], in0=ctmp[:, 0:nq], in1=fm[:, 6 + c, t0:t0 + nq], op=ALU.mult))
            chunks = [(c0, min(512, nkeys - c0)) for c0 in range(0, nkeys, 512)]
            steps = [(ci, h) for ci in range(len(chunks)) for h in range(8)]
            pend = []

            def fin(i):
                ci, h = steps[i]
                c0, w = chunks[ci]
                ps, rr = pend[i]
                acc = pbank[4 + ci % 2]
                A([ps], [rr], lambda e: e.activation(out=rr[0:nq, 0:w], in_=ps[0:nq, 0:w], func=AF.Relu))
                T([Dm, rr], [acc], lambda e: e.matmul(acc[0:nq, 0:w], Dm[0:nq, h, 0:nq], rr[0:nq, 0:w],
                                                     start=(h == 0), stop=(h == 7)))
                if h == 7:
                    A([acc], [sc], lambda e: e.activation(out=sc[0:nq, c0:c0 + w], in_=acc[0:nq, 0:w], func=AF.Copy))
            for i, (ci, h) in enumerate(steps):
                c0, w = chunks[ci]
                ps = R(); rr = rh[i % 4]
                pend.append((ps, rr))
                hp = 64 * (h % 2)
                T([qiT, kiT], [ps], lambda e, ps=ps, hp=hp, h=h, c0=c0, w=w: e.matmul(
                    ps[0:nq, 0:w], qiT[hp:hp + 64, h // 2, 0:nq], kiT[hp:hp + 64, c0:c0 + w], start=True, stop=True))
                if i >= 2:
                    fin(i - 2)
            for i in range(max(0, len(steps) - 2), len(steps)):
                fin(i)
            ksel = min(256, (S if prm else PAST + 4) // 4)
            V([sc], [sm], lambda e: e.tensor_reduce(out=sm[0:nq, 24:25], in_=sc[0:nq, 0:nkeys], axis=mybir.AxisListType.X, op=ALU.max))
            V([sc], [sm], lambda e: e.tensor_reduce(out=sm[0:nq, 25:26], in_=sc[0:nq, 0:nkeys], axis=mybir.AxisListType.X, op=ALU.min))
            V([sc, tneg], [sc], lambda e: e.tensor_tensor(out=sc[0:nq, koff:koff + nq], in0=sc[0:nq, koff:koff + nq],
                                                          in1=tneg[0:nq, 0:nq], op=ALU.add))
            V([sm], [sm], lambda e: e.scalar_tensor_tensor(out=sm[0:nq, 26:27], in0=sm[0:nq, 24:25], scalar=1.0,
                                                           in1=sm[0:nq, 25:26], op0=ALU.add, op1=ALU.subtract))
            lo = sm[0:nq, 25:26]; w0 = sm[0:nq, 26:27]; mid = sm[0:nq, 27:28]; cnt = sm[0:nq, 28:29]; tmp = sm[0:nq, 29:30]
            for it in range(NIT):
                cc = 2.0 ** -(it + 1)
                V([sm], [sm], lambda e, cc=cc: e.scalar_tensor_tensor(out=mid, in0=w0, scalar=cc, in1=lo, op0=ALU.mult, op1=ALU.add))
                for s0 in range(0, nkeys, 2048):
                    sw = min(2048, nkeys - s0)
                    V([sc, sm], [junk, sm], lambda e, s0=s0, sw=sw: e.tensor_scalar(
                        out=junk[0:nq, 0:sw], in0=sc[0:nq, s0:s0 + sw], scalar1=mid, scalar2=(None if s0 == 0 else cnt),
                        op0=ALU.is_ge, op1=ALU.add, accum_out=cnt))
                V([sm], [sm], lambda e: e.tensor_scalar(out=tmp, in0=cnt, scalar1=ksel - 0.5, scalar2=w0, op0=ALU.is_ge, op1=ALU.mult))
                V([sm], [sm], lambda e, cc=cc: e.scalar_tensor_tensor(out=lo, in0=tmp, scalar=cc, in1=lo, op0=ALU.mult, op1=ALU.add))
            if prm:
                kcs = [(kc, kc * 128, 128) for kc in range(koff // 128 + 1)]
            else:
                kcs = [(kc, kc * 128, 128) for kc in range(NPG)] + [(NPG, PAST, 4)]
            og = [pbank[4], pbank[5]]
            N4 = 4 * nq
            asteps = [(a, g) for a in range(len(kcs)) for g in range(2)]
            apend = []

            def afin(i):
                a, g = asteps[i]
                kc, k0, kw = kcs[a]
                st, pt_ = apend[i]
                A([st], [pt_], lambda e: e.activation(out=pt_[0:kw, 0:N4], in_=st[0:kw, 0:N4], func=AF.Exp, scale=0.125))
                T([Vp, pt_], [og[g]], lambda e: e.matmul(og[g][0:65, 0:N4], Vp[0:kw, kc, g, 0:65], pt_[0:kw, 0:N4],
                                                        start=(a == 0), stop=(a == len(kcs) - 1)))
            for i, (a, g) in enumerate(asteps):
                kc, k0, kw = kcs[a]
                if g == 0:
                    nm_ = nmc[a % 4]
                    V([sc, sm], [nm_], lambda e, nm_=nm_, k0=k0, kw=kw: e.tensor_scalar(
                        out=nm_[0:nq, 0:kw], in0=sc[0:nq, k0:k0 + kw], scalar1=lo, scalar2=NEG, op0=ALU.is_lt, op1=ALU.mult))
                st = R(); pt_ = pT[i % 4]
                apend.append((st, pt_))
                st3 = st[0:kw, 0:N4].rearrange("p (r t) -> p r t", r=4)
                T([kT, qT], [st], lambda e, st3=st3, g=g, k0=k0, kw=kw: e.matmul(
                    st3, kT[64 * g:64 * g + 64, k0:k0 + kw], qT[64 * g:64 * g + 64, :, 0:nq], start=True, stop=False))
                T([nm_, identrep], [st], lambda e, st3=st3, nm_=nm_, kw=kw: e.matmul(
                    st3, nm_[0:nq, 0:kw], identrep[0:nq, :, 0:nq], start=False, stop=True))
                if i >= 2:
                    afin(i - 2)
            for i in range(max(0, len(asteps) - 2), len(asteps)):
                afin(i)
            for g in range(2):
                A([og[g]], [srow], lambda e, g=g: e.activation(out=srow[64:65, 0:N4], in_=og[g][64:65, 0:N4], func=AF.Copy))
                pbs = R()
                T([onesf, srow], [pbs], lambda e, pbs=pbs: e.matmul(pbs[0:64, 0:N4], onesf[64:65, 0:64], srow[64:65, 0:N4],
                                                                   start=True, stop=True))
                V([pbs], [rs], lambda e, pbs=pbs: e.reciprocal(out=rs[0:64, 0:N4], in_=pbs[0:64, 0:N4]))
                V([og[g], rs], [oT], lambda e, g=g: e.tensor_tensor(
                    out=oT[0:64, g, :, t0:t0 + nq], in0=og[g][0:64, 0:N4].rearrange("p (r t) -> p r t", r=4),
                    in1=rs[0:64, 0:N4].rearrange("p (r t) -> p r t", r=4), op=ALU.mult))

        stage = sc[:, 0:2816]
        cvb = sc[:, 2816:2816 + 1408].bitcast(BF16)
        cv_n = [0]

        def conv_piece(src_ap, w, outs, key):
            DS([], [sc], lambda e: e.dma_start(out=stage[:, 0:w], in_=src_ap))
            k_ = cv_n[0] % 3; cv_n[0] += 1
            if k_ == 0:
                A([sc], [sc], lambda e: e.activation(out=cvb[:, 0:w], in_=stage[:, 0:w], func=AF.Copy))
            elif k_ == 1:
                V([sc], [sc], lambda e: e.tensor_copy(out=cvb[:, 0:w], in_=stage[:, 0:w]))
            else:
                G([sc], [sc], lambda e: e.tensor_copy(out=cvb[:, 0:w], in_=stage[:, 0:w]))
            for (dst, src) in outs:
                DS([sc], [key], lambda e, dst=dst, src=src: e.dma_start(out=dst, in_=src))
        for l in range(L):
            for k in range(8):
                conv_piece(w_in[l, k * 128:(k + 1) * 128, :], WCOLS, [
                    (wis[l, 0:9, :, k, :].rearrange("j p c -> p j c"), cvb[:, 0:2304].rearrange("p (j c) -> p j c", c=256)),
                    (wis[l, 9, :, k, 0:136], cvb[:, 2304:2440]),
                    (wis[l, 10, :, k, :], cvb[:, 2440:2696])], ("w", "in", l))
            for r in range(8):
                if r < 2 or r >= 6:
                    kc = r if r < 2 else 10 + (r - 6)
                    outs = [(wos[l, :, :, kc, :].rearrange("o p c -> p o c"), cvb[:, 0:1024].rearrange("p (o c) -> p o c", c=128))]
                else:
                    hA = 2 * (r - 2)
                    outs = [(wos[l, :, 0:64, 2 + hA, :].rearrange("o p c -> p o c"), cvb[0:64, 0:1024].rearrange("p (o c) -> p o c", c=128)),
                            (wos[l, :, 0:64, 3 + hA, :].rearrange("o p c -> p o c"), cvb[64:128, 0:1024].rearrange("p (o c) -> p o c", c=128))]
                conv_piece(w_o[l, r * 128:(r + 1) * 128, :], 1024, outs, ("w", "o", l))
            for k in range(8):
                for gu in range(2):
                    conv_piece(w_up[l, k * 128:(k + 1) * 128, gu * DFF:(gu + 1) * DFF], DFF, [
                        (wus[l, :, :, k, gu, :].rearrange("i p c -> p i c"), cvb[:, 0:DFF].rearrange("p (i c) -> p i c", c=128))],
                        ("w", "up", l))
            for i in range(22):
                conv_piece(w_dn[l, i * 128:(i + 1) * 128, :], 1024, [
                    (wds[l, :, i // 11, :, i % 11, :].rearrange("o p c -> p o c"), cvb[:, 0:1024].rearrange("p (o c) -> p o c", c=128))],
                    ("w", "dn", l))

        for l in range(L):
            dma_in("s", lncv_t, lncv_t[:, :], lncv[:, l * 512:(l + 1) * 512])
            dma_in("s", bspc_t, bspc_t[:, :], bspc[:, l * 256:(l + 1) * 256])
            dma_in("s", wtmp, wtmp[:, :], wspT[:, l * 512:l * 512 + 128])
            for h in range(4):
                if h > 0:
                    dma_in("s", wtmp, wtmp[:, :], wspT[:, (l * 4 + h) * 128:(l * 4 + h + 1) * 128])
                V([wtmp, triu], [wmT], lambda e, h=h: e.tensor_tensor(out=wmT[:, h, :], in0=wtmp[:, :], in1=triu[:, :], op=ALU.mult))
            for sbi in range(S // NTP):
                process_sb(l, "p", sbi)
            process_sb(l, "s", 0)
        P.finish()

        with nc.Block() as block:
            @block.tensor
            def _(e):
                for f in P.q["tensor"]:
                    f(e)

            @block.vector
            def _(e):
                for f in P.q["vector"]:
                    f(e)

            @block.scalar
            def _(e):
                for f in P.q["scalar"]:
                    f(e)

            @block.gpsimd
            def _(e):
                for f in P.q["gpsimd"]:
                    f(e)

            @block.sync
            def _(e):
                for f in P.q["sync"]:
                    f(e)
    return nc, P.n


def _host_layout(inputs, cfg, core):
    S, L, NB, NPG, NPOOL = cfg["S"], cfg["L"], cfg["NB"], cfg["NPG"], cfg["NPOOL"]
    f = lambda a: np.ascontiguousarray(a, dtype=np.float32)
    w_in = np.asarray(inputs["w_in"])
    qperm = np.concatenate([768 + h * 64 + np.arange(64) for h in (0, 4, 1, 5, 2, 6, 3, 7)])
    r = np.arange
    cols = np.concatenate([r(0, 256), r(256, 512), r(512, 768), r(2120, 2376), qperm, r(1536, 2048),
                           r(1280, 1408), r(1408, 1536), r(2048, 2112), r(2048, 2112), r(2112, 2120), r(2376, 2632)])
    assert cols.size == WCOLS
    m = {}
    m["xT"] = f(np.asarray(inputs["x_prompt"])[core].T)
    xs = np.asarray(inputs["x_sample"])[core * NB:(core + 1) * NB].reshape(NB * 4, D)
    m["xsT"] = f(xs.T)
    m["w_in"] = f(w_in[:, :, cols])
    m["w_o"] = f(inputs["w_o"]); m["w_up"] = f(inputs["w_up"]); m["w_dn"] = f(inputs["w_down"])
    ca = np.asarray(inputs["conv_a"])
    m["cva"] = f(ca.reshape(L, 3, 2, 128).transpose(3, 0, 2, 1).reshape(128, L * 6))
    cf = np.asarray(inputs["conv_f"])
    m["cvf"] = f(cf.reshape(L, 3, NCH_F, 128).transpose(3, 0, 2, 1).reshape(128, L * NCH_F * 3))
    ln = np.stack([np.asarray(inputs[k]) for k in ("ln1_g", "ln1_b", "ln2_g", "ln2_b")], axis=1)
    m["lng"] = f(ln.reshape(L, 4, 8, 128).transpose(3, 0, 1, 2).reshape(128, L * 32))
    lcv = np.stack([np.asarray(inputs["ln_cv_g"]), np.asarray(inputs["ln_cv_b"])], axis=1)
    m["lncv"] = f(np.broadcast_to(lcv.reshape(1, L * 512), (128, L * 512)))
    bs = np.asarray(inputs["b_sp"])
    bc = bs.reshape(L, 2, 2, 128)
    m["bspc"] = f(np.repeat(bc.transpose(2, 0, 1, 3), 64, axis=0).reshape(128, L * 256))
    ws = np.asarray(inputs["w_sp"])
    m["wspT"] = f(ws.transpose(3, 0, 1, 2).reshape(128, L * 512))
    half = 8
    inv = (np.float32(500000.0) ** (-np.arange(half, dtype=np.float32) / np.float32(half))).astype(np.float32)
    def cstab(pos):
        ang = pos.astype(np.float32)[:, None] * inv[None, :]
        return np.concatenate([np.cos(ang), np.sin(ang)], axis=1).astype(np.float32)
    m["csp"] = cstab(np.arange(S))
    m["css"] = cstab(NPG * 128 + np.arange(4))
    m["cident"] = np.eye(128, dtype=np.float32)
    ii = np.arange(128)
    m["ctriu"] = (ii[:, None] <= ii[None, :]).astype(np.float32)
    m["ctneg"] = np.where(ii[None, :] <= ii[:, None], 0.0, -1e30).astype(np.float32)
    m["ciota"] = ii.astype(np.float32).reshape(128, 1)
    m["ck"] = f(np.asarray(inputs["cache_k"]).reshape(L * NPOOL * 128, 128))
    m["cv"] = f(np.asarray(inputs["cache_v"]).reshape(L * NPOOL * 128, 128))
    m["cki"] = f(np.asarray(inputs["cache_kidx"]).reshape(L * NPOOL * 128, 64))
    m["pt"] = np.ascontiguousarray(np.asarray(inputs["page_table"])[core * NB:(core + 1) * NB].reshape(1, NB * NPG), dtype=np.int32)
    sc_ = np.asarray(inputs["state_conv"])[:, core * NB:(core + 1) * NB]
    m["stcT"] = f(sc_.transpose(0, 3, 1, 2).reshape(L, 256, NB * 2))
    sf_ = np.asarray(inputs["state_ffn_conv"])[:, core * NB:(core + 1) * NB]
    m["stfT"] = f(sf_.transpose(0, 3, 1, 2).reshape(L, 2 * DFF, NB * 2))
    return m


_CACHE = {}


def run(inputs, cfg, ncores=2):
    key = tuple(sorted(cfg.items()))
    if key not in _CACHE:
        _CACHE[key] = build(cfg)[0]
    nc = _CACHE[key]
    maps = [_host_layout(inputs, cfg, c) for c in range(ncores)]
    res = run_bass_kernel_spmd(nc, maps, core_ids=list(range(ncores))).results
    S, L, NB = cfg["S"], cfg["L"], cfg["NB"]
    cat = lambda k, ax: np.concatenate([np.asarray(r[k]) for r in res], axis=ax)
    st = lambda k: np.stack([np.asarray(r[k]) for r in res], axis=0)
    y_p = st("y_p")
    y_s = cat("y_s", 0).reshape(ncores * NB, 4, D)
    nk_p = st("nk_p").transpose(1, 0, 2, 3).reshape(L, ncores, S, 2, 64)
    nv_p = st("nv_p").transpose(1, 0, 2, 3).reshape(L, ncores, S, 2, 64)
    nki_p = st("nki_p").transpose(1, 0, 2, 3)
    nk_s = cat("nk_s", 1).reshape(L, ncores * NB, 4, 2, 64)
    nv_s = cat("nv_s", 1).reshape(L, ncores * NB, 4, 2, 64)
    nki_s = cat("nki_s", 1).reshape(L, ncores * NB, 4, 64)
    ncv_p = st("ncv_p").transpose(1, 0, 3, 2)
    ncv_s = cat("ncv_s", 2).reshape(L, 256, ncores * NB, 2).transpose(0, 2, 3, 1)
    nff_p = st("nff_p").transpose(1, 0, 3, 2)
    nff_s = cat("nff_s", 2).reshape(L, 2 * DFF, ncores * NB, 2).transpose(0, 2, 3, 1)
    nch_s = cat("nch_s", 1).reshape(L, ncores * NB, 4, 256)
    outs = (y_p, y_s, nk_p, nk_s, nv_p, nv_s, nki_p, nki_s, ncv_p, ncv_s, nff_p, nff_s, nch_s)
    return tuple(np.ascontiguousarray(o, dtype=np.float32) for o in outs)


def kernel(**inputs):
    xp = np.asarray(inputs["x_prompt"])
    ptab = np.asarray(inputs["page_table"])
    cfg = dict(S=xp.shape[1], L=np.asarray(inputs["w_in"]).shape[0], NB=ptab.shape[0] // 2,
               NPG=ptab.shape[1], NPOOL=np.asarray(inputs["cache_k"]).shape[1])
    return run(inputs, cfg, ncores=2)
```

```python
import sys
import numpy as np
from contextlib import ExitStack
import concourse.bass as bass
import concourse.mybir as mybir
from concourse.bass_utils import run_bass_kernel_spmd

F32 = mybir.dt.float32
BF16 = mybir.dt.bfloat16
I32 = mybir.dt.int32
AF = mybir.ActivationFunctionType
ALU = mybir.AluOpType

D = 1024
DFF = 2816
NCH_F = 44
ALPHA = 8.0 ** 0.25
LN_EPS = 1e-5
NIT = 18
KSEL = 256
NEG = -30000.0
WCOLS = 2696


class Buf:
    __slots__ = ("w", "r")

    def __init__(self):
        self.w = None
        self.r = {}


class Prog:
    CE = ("tensor", "vector", "scalar", "gpsimd")

    def __init__(self, nc, es, ndma=12):
        self.nc = nc
        self.q = {e: [] for e in ("tensor", "vector", "scalar", "gpsimd", "sync")}
        self.semh = {}
        self.cnt = {}
        self.waited = {e: {} for e in self.q}
        for e in self.CE:
            self.semh[e] = es.enter_context(nc.semaphore("s_" + e))
            self.cnt[e] = 0
        self.dsem = {}
        self.dcnt = {}
        self.dnext = {}
        for qn in ("sync", "gpsimd"):
            self.dsem[qn] = []
            for i in range(ndma):
                k = "d_%s_%d" % (qn, i)
                self.semh[k] = es.enter_context(nc.semaphore(k))
                self.dsem[qn].append(k)
                self.dcnt[k] = 0
            self.dnext[qn] = 0
        self.n = 0
        self.nops = 0

    def _waits(self, eng, reads, writes):
        need = {}
        for b in reads:
            if b.w is not None:
                s, v = b.w
                if need.get(s, 0) < v:
                    need[s] = v
        for b in writes:
            if b.w is not None:
                s, v = b.w
                if need.get(s, 0) < v:
                    need[s] = v
            for s, v in b.r.items():
                if need.get(s, 0) < v:
                    need[s] = v
        out = []
        wd = self.waited[eng]
        for s, v in need.items():
            if eng == "tensor" and s == "tensor":
                continue
            if wd.get(s, 0) >= v:
                continue
            wd[s] = v
            out.append((s, v))
        return out

    def _mark(self, ev, reads, writes):
        s, v = ev
        for b in reads:
            if b.r.get(s, 0) < v:
                b.r[s] = v
        for b in writes:
            b.w = ev
            b.r = {}

    LIMIT = None
    MARKS = []
    LOG = []

    def op(self, eng, reads, writes, fn):
        if Prog.LIMIT is not None and self.nops >= Prog.LIMIT:
            return
        self.nops += 1
        Prog.LOG.append((self.nops, eng, sys._getframe(2).f_lineno))
        waits = self._waits(eng, reads, writes)
        self.cnt[eng] += 1
        ev = (eng, self.cnt[eng])
        semh = self.semh
        sh = semh[eng]

        def run(e):
            for s, v in waits:
                e.wait_ge(semh[s], v)
            fn(e).then_inc(sh, 1)
        self.q[eng].append(run)
        self._mark(ev, reads, writes)
        self.n += 1 + len(waits)

    def dma(self, qn, reads, writes, fn):
        if Prog.LIMIT is not None and self.nops >= Prog.LIMIT:
            return
        self.nops += 1
        Prog.LOG.append((self.nops, 'dma-' + qn, sys._getframe(2).f_lineno))
        waits = self._waits(qn, reads, writes)
        i = self.dnext[qn]
        self.dnext[qn] = (i + 1) % len(self.dsem[qn])
        k = self.dsem[qn][i]
        prev = self.dcnt[k]
        if prev > 0 and self.waited[qn].get(k, 0) < 16 * prev:
            waits.append((k, 16 * prev))
            self.waited[qn][k] = 16 * prev
        self.dcnt[k] = prev + 1
        ev = (k, 16 * (prev + 1))
        semh = self.semh
        sh = semh[k]

        def run(e):
            for s, v in waits:
                e.wait_ge(semh[s], v)
            fn(e).then_inc(sh, 16)
        self.q[qn].append(run)
        self._mark(ev, reads, writes)
        self.n += 1 + len(waits)

    def finish(self):
        waits = [(k, 16 * c) for k, c in self.dcnt.items() if c > 0]
        semh = self.semh

        def run(e):
            for s, v in waits:
                e.wait_ge(semh[s], v)
        self.q["sync"].append(run)


def build(cfg):
    S, L, NB, NPG, NPOOL = cfg["S"], cfg["L"], cfg["NB"], cfg["NPG"], cfg["NPOOL"]
    PAST = NPG * 128
    NTP = 256
    NTS = NB * 4
    KMAX = max(S, PAST + 4)
    NKC = max(S // 128, NPG + 1)
    nc = bass.Bass("TRN2", target_bir_lowering=False)

    def din(name, shape, dt=F32):
        return nc.dram_tensor(name, list(shape), dt, kind="ExternalInput")

    def dout(name, shape, dt=F32):
        return nc.dram_tensor(name, list(shape), dt, kind="ExternalOutput")

    xT = din("xT", [D, S])
    xsT = din("xsT", [D, NTS])
    w_in = din("w_in", [L, D, WCOLS])
    w_o = din("w_o", [L, D, D])
    w_up = din("w_up", [L, D, 2 * DFF])
    w_dn = din("w_dn", [L, DFF, D])
    cva = din("cva", [128, L * 2 * 3])
    cvf = din("cvf", [128, L * NCH_F * 3])
    lng = din("lng", [128, L * 4 * 8])
    lncv = din("lncv", [128, L * 2 * 256])
    bspc = din("bspc", [128, L * 2 * 128])
    wspT = din("wspT", [128, L * 4 * 128])
    csp = din("csp", [S, 16])
    css = din("css", [4, 16])
    cident = din("cident", [128, 128])
    ctriu = din("ctriu", [128, 128])
    ctneg = din("ctneg", [128, 128])
    ciota = din("ciota", [128, 1])
    ckv = din("ckv", [L * NPOOL * 128, 320])
    pt = din("pt", [1, NB * NPG], I32)
    stcT = din("stcT", [L, 256, NB * 2])
    stfT = din("stfT", [L, 2 * DFF, NB * 2])

    y_p = dout("y_p", [S, D])
    y_s = dout("y_s", [NTS, D])
    nk_p = dout("nk_p", [L, S, 128])
    nk_s = dout("nk_s", [L, NTS, 128])
    nv_p = dout("nv_p", [L, S, 128])
    nv_s = dout("nv_s", [L, NTS, 128])
    nki_p = dout("nki_p", [L, S, 64])
    nki_s = dout("nki_s", [L, NTS, 64])
    ncv_p = dout("ncv_p", [L, 256, 2])
    ncv_s = dout("ncv_s", [L, 256, NB * 2])
    nff_p = dout("nff_p", [L, 2 * DFF, 2])
    nff_s = dout("nff_s", [L, 2 * DFF, NB * 2])
    nch_s = dout("nch_s", [L, NTS, 256])
    hsc_p = [nc.dram_tensor("hsc_p%d" % i, [D, S], F32, kind="Internal") for i in range(2)]
    hsc_s = [nc.dram_tensor("hsc_s%d" % i, [D, NTS], F32, kind="Internal") for i in range(2)]
    wis = nc.dram_tensor("wis", [L, 11, 128, 8, 256], BF16, kind="Internal")
    wos = nc.dram_tensor("wos", [L, 8, 128, 12, 128], BF16, kind="Internal")
    wus = nc.dram_tensor("wus", [L, 22, 128, 8, 2, 128], BF16, kind="Internal")
    wds = nc.dram_tensor("wds", [L, 8, 2, 128, 11, 128], BF16, kind="Internal")

    es = ExitStack()
    with es:
        P = Prog(nc, es)

        def sb(name, shape, dt=F32):
            return es.enter_context(nc.sbuf_tensor(name, list(shape), dt))

        ident = sb("ident", [128, 128]); identb = sb("identb", [128, 128], BF16)
        identrep = sb("identrep", [128, 4, 128], BF16)
        onesf = sb("onesf", [128, 128])
        triu = sb("triu", [128, 128]); tneg = sb("tneg", [128, 128])
        iota = sb("iota", [128, 1])
        cva_t = sb("cva_t", [128, L * 6]); cvf_t = sb("cvf_t", [128, L * NCH_F * 3])
        lng_t = sb("lng_t", [128, L * 32]); lncv_t = sb("lncv_t", [128, 512])
        bspc_t = sb("bspc_t", [128, 256])
        wmT = sb("wmT", [128, 4, 128], BF16)
        wtmp = sb("wtmp", [128, 128])
        idx = sb("idx", [128, NPG], I32)
        iol = sb("iol", [128, L])
        ptb = sb("ptb", [128, NPG], I32)
        kT = sb("kT", [128, KMAX], BF16)
        kiT = sb("kiT", [128, KMAX], BF16)
        Vp = sb("Vp", [128, NKC, 2, 65], BF16)
        sc = sb("sc", [128, max(KMAX, 4224)])
        junk = sb("junk", [128, 2048], mybir.dt.uint8)
        xt = sb("xt", [128, 8, NTP]); xb = sb("xb", [128, 8, NTP], BF16)
        fm = sb("fm", [128, 8, NTP])
        zt = sb("zt", [128, 2, NTP + 2 * NB])
        hl = sb("hl", [128, NCH_F, 2])
        stfc = [sb("stfc%d" % i, [128, NB * 2]) for i in range(4)]
        aT = sb("aT", [128, 2, NTP], BF16); cT = sb("cT", [128, 2, NTP], BF16)
        oT = sb("oT", [64, 2, 4, NTP], BF16)
        qT = sb("qT", [128, 4, 128], BF16); qiT = sb("qiT", [128, 4, 128], BF16)
        qq = sb("qq", [128, 1024]); qqb = sb("qqb", [128, 1024], BF16)
        kv = sb("kv", [128, 392]); kvb = sb("kvb", [128, 392], BF16)
        rt = sb("rt", [128, 4, 16, 8])
        cs = sb("cs", [128, 16])
        sm = sb("sm", [128, 64])
        Dm = sb("Dm", [128, 8, 128], BF16)
        vn = sb("vn", [128, 256]); vnb = sb("vnb", [128, 256], BF16)
        bst = sb("bst", [128, 8])
        rh = [sb("rh%d" % i, [128, 512], BF16) for i in range(4)]
        pT = [sb("pT%d" % i, [128, 512], BF16) for i in range(4)]
        nmc = [sb("nmc%d" % i, [128, 128], BF16) for i in range(4)]
        srow = sb("srow", [128, 512]); rs = sb("rs", [64, 512])
        winb = [sb("winb%d" % i, [128, 8, 256], BF16) for i in range(3)]
        wob = [sb("wob%d" % i, [128, 12, 128], BF16) for i in range(2)]
        wupb = [sb("wupb%d" % i, [128, 8, 2, 128], BF16) for i in range(2)]
        wdnb = [sb("wdnb%d" % i, [128, 11, 128], BF16) for i in range(3)]
        hs = [sb("hs%d" % i, [128, NTP + 2 * NB]) for i in range(4)]
        yv = [sb("yv%d" % i, [128, NTP]) for i in range(4)]
        mt = sb("mt", [128, NTP]); rt2 = sb("rt2", [128, NTP]); msq = sb("msq", [128, NTP])
        ctmp = sb("ctmp", [128, 128])
        pgs = [sb("pgs%d" % i, [128, 320]) for i in range(2)]
        kpb = [sb("kpb%d" % i, [128, 128], BF16) for i in range(2)]
        ipb = [sb("ipb%d" % i, [128, 128], BF16) for i in range(2)]
        pbank = [es.enter_context(nc.psum_tensor("pb%d" % i, [128, 512], F32)) for i in range(6)]
        pbank.append(es.enter_context(nc.psum_tensor("pb6", [128, 8, 128], BF16)))
        pbank.append(es.enter_context(nc.psum_tensor("pb7", [128, 8, 128], BF16)))
        B = {}

        def buf(o):
            if isinstance(o, Buf):
                return o
            k = o if isinstance(o, (tuple, str)) else id(o)
            if k not in B:
                B[k] = Buf()
            return B[k]
        Rn = [0]

        def R():
            i = Rn[0] % 4
            Rn[0] += 1
            return pbank[i]
        ptb_bf = pbank[7]
        pt2_bf = pbank[6]

        psum_ids = set(id(x) for x in pbank)

        def _rw(r, w):
            rr = [buf(x) for x in r if id(x) not in psum_ids]
            ww = [buf(x) for x in w] + [buf(x) for x in r if id(x) in psum_ids]
            return rr, ww
        def V(r, w, fn): P.op("vector", *_rw(r, w), fn)
        def A(r, w, fn): P.op("scalar", *_rw(r, w), fn)
        def G(r, w, fn): P.op("gpsimd", *_rw(r, w), fn)
        def T(r, w, fn): P.op("tensor", *_rw(r, w), fn)
        def DS(r, w, fn): P.dma("sync", [buf(x) for x in r], [buf(x) for x in w], fn)
        def DG(r, w, fn): P.dma("gpsimd", [buf(x) for x in r], [buf(x) for x in w], fn)

        def dma_in(q, dst_t, dst_ap, src_ap, rd=()):
            (DS if q == "s" else DG)(list(rd), [dst_t], lambda e: e.dma_start(out=dst_ap, in_=src_ap))

        def dma_out(q, dst_ap, src_t, src_ap, wr=()):
            (DS if q == "s" else DG)([src_t], list(wr), lambda e: e.dma_start(out=dst_ap, in_=src_ap))

        dma_in("s", ident, ident[:, :], cident[:, :])
        dma_in("s", triu, triu[:, :], ctriu[:, :])
        dma_in("s", tneg, tneg[:, :], ctneg[:, :])
        dma_in("s", iota, iota[:, :], ciota[:, :])
        dma_in("s", cva_t, cva_t[:, :], cva[:, :])
        dma_in("s", cvf_t, cvf_t[:, :], cvf[:, :])
        dma_in("s", lng_t, lng_t[:, :], lng[:, :])
        V([ident], [identb], lambda e: e.tensor_copy(out=identb[:, :], in_=ident[:, :]))
        for r in range(4):
            V([ident], [identrep], lambda e, r=r: e.tensor_copy(out=identrep[:, r, :], in_=ident[:, :]))
        V([], [onesf], lambda e: e.memset(onesf[:, :], 1.0))
        V([], [Vp], lambda e: e.memset(Vp[:, :, :, 64:65], 1.0))
        for l_ in range(L):
            V([iota], [iol], lambda e, l_=l_: e.tensor_scalar(out=iol[:, l_:l_ + 1], in0=iota[:, 0:1], scalar1=float(l_ * NPOOL * 128),
                                                              scalar2=None, op0=ALU.add))

        wi_n = [0]
        wd_n = [0]

        def mm_group(out_t, out_ap, pairs, rd):
            n = len(pairs)
            for i, (l_, r_) in enumerate(pairs):
                T(rd, [out_t],
                  lambda e, l_=l_, r_=r_, i=i: e.matmul(out_ap, l_, r_, start=(i == 0), stop=(i == n - 1)))

        def process_sb(l, kind, sbi):
            Prog.MARKS.append(('sb', l, kind, sbi, P.nops))
            prm = kind == "p"
            NT = NTP if prm else NTS
            tok0 = sbi * NTP if prm else 0
            nq = 128 if prm else 4
            ntm = NT // nq
            if l == 0:
                src = xT if prm else xsT
            else:
                src = (hsc_p if prm else hsc_s)[(l - 1) % 2]
            srcb = ("h", prm, (l - 1) % 2, sbi)
            dma_in("s", xt, xt[:, :, 0:NT], src[:, tok0:tok0 + NT].rearrange("(c p) t -> p c t", p=128),
                   rd=[srcb] if l > 0 else [])
            A([xt], [xb], lambda e: e.activation(out=xb[:, :, 0:NT], in_=xt[:, :, 0:NT], func=AF.Copy))
            for half in range(4):
                wb = winb[wi_n[0] % 3]; wi_n[0] += 1
                dma_in("s", wb, wb[:, :, :], wis[l, half, :, :, :], rd=[("w", "in", l)])
                for j in range(2):
                    oc = half * 2 + j
                    ps = R()
                    mm_group(ps, ps[:, 0:NT], [(wb[:, k, j * 128:(j + 1) * 128], xb[:, k, 0:NT]) for k in range(8)], [wb, xb])
                    A([ps], [fm], lambda e, ps=ps, oc=oc: e.activation(out=fm[:, oc, 0:NT], in_=ps[:, 0:NT], func=AF.Copy))
            if prm:
                zv = zt[:, :, 0:NT + 2].rearrange("p c (b t) -> p c b t", b=1)
                if sbi == 0:
                    V([], [zt], lambda e: e.memset(zt[:, :, 0:2], 0.0))
                else:
                    A([zt], [zt], lambda e: e.activation(out=zt[:, :, 0:2], in_=zt[:, :, NT:NT + 2], func=AF.Copy))
                nbk, tt = 1, NT
            else:
                zv = zt[:, :, 0:NB * 6].rearrange("p c (b t) -> p c b t", b=NB)
                for c_ in range(2):
                    DS([], [buf(zt)], lambda e, c_=c_: e.dma_start(
                        out=zv[:, c_, :, 0:2], in_=stcT[l, c_ * 128:(c_ + 1) * 128, :].rearrange("p (b i) -> p b i", i=2)))
                nbk, tt = NB, 4
            fmv = fm[:, :, 0:NT].rearrange("p c (b t) -> p c b t", b=nbk)
            V([fm, zt], [zt], lambda e: e.tensor_tensor(out=zv[:, :, :, 2:2 + tt], in0=fmv[:, 2:4, :, :],
                                                         in1=fmv[:, 4:6, :, :], op=ALU.mult))
            if prm and sbi == S // NTP - 1:
                dma_out("s", ncv_p[l, :, :].rearrange("(c p) i -> p c i", p=128), zt, zt[:, :, NT:NT + 2])
            if not prm:
                for c_ in range(2):
                    dma_out("s", ncv_s[l, c_ * 128:(c_ + 1) * 128, :].rearrange("p (b i) -> p b i", i=2), zt, zv[:, c_, :, 4:6])
            for c in range(2):
                yy = yv[c]
                yyv = yy[:, 0:NT].rearrange("p (b t) -> p b t", b=nbk)
                w = lambda i, c=c: cva_t[:, (l * 2 + c) * 3 + i:(l * 2 + c) * 3 + i + 1]
                V([zt, cva_t], [yy], lambda e, c=c, yyv=yyv, w=w: e.tensor_scalar(
                    out=yyv, in0=zv[:, c, :, 2:2 + tt], scalar1=w(2), scalar2=None, op0=ALU.mult))
                V([zt, yy], [yy], lambda e, c=c, yyv=yyv, w=w: e.scalar_tensor_tensor(
                    out=yyv, in0=zv[:, c, :, 1:1 + tt], scalar=w(1), in1=yyv, op0=ALU.mult, op1=ALU.add))
                V([zt, yy], [yy], lambda e, c=c, yyv=yyv, w=w: e.scalar_tensor_tensor(
                    out=yyv, in0=zv[:, c, :, 0:tt], scalar=w(0), in1=yyv, op0=ALU.mult, op1=ALU.add))
                V([yy, fm], [aT], lambda e, c=c, yy=yy: e.tensor_tensor(
                    out=aT[:, c, 0:NT], in0=yy[:, 0:NT], in1=fm[:, c, 0:NT], op=ALU.mult))
            for tm in range(ntm):
                t0 = tm * nq
                g0 = tok0 + t0
                if prm:
                    jblk = g0 // 128
                    koff = jblk * 128
                    vchunk = jblk
                    nkeys = koff + 128
                else:
                    koff = PAST
                    vchunk = NPG
                    nkeys = PAST + 4
                    Prog.MARKS.append(('pages', l, tm, P.nops))
                    stage_pages(l, tm)
                tm_block(l, prm, NT, nq, t0, g0, koff, vchunk, nkeys)
            for oc in range(8):
                wb = wob[oc % 2]
                dma_in("s", wb, wb[:, :, :], wos[l, oc, :, :, :], rd=[("w", "o", l)])
                pairs = [(wb[:, c, :], aT[:, c, 0:NT]) for c in range(2)]
                pairs += [(wb[0:64, 2 + h, :], oT[0:64, h // 4, h % 4, 0:NT]) for h in range(8)]
                pairs += [(wb[:, 10 + c, :], cT[:, c, 0:NT]) for c in range(2)]
                ps = R()
                mm_group(ps, ps[:, 0:NT], pairs, [wb, aT, oT, cT])
                V([ps, xt], [xt], lambda e, ps=ps, oc=oc: e.scalar_tensor_tensor(
                    out=xt[:, oc, 0:NT], in0=xt[:, oc, 0:NT], scalar=ALPHA, in1=ps[:, 0:NT], op0=ALU.mult, op1=ALU.add))
            layer_norm(l, 0, NT)
            A([xt], [xb], lambda e: e.activation(out=xb[:, :, 0:NT], in_=xt[:, :, 0:NT], func=AF.Copy))
            actT = sc[:, 0:11 * NTP].bitcast(BF16)
            hview = lambda t_: t_[:, 0:(NT + 2 * nbk)].rearrange("p (b t) -> p b t", b=nbk)
            for i in range(22):
                wb = wupb[i % 2]
                dma_in("s", wb, wb[:, :, :, :], wus[l, i, :, :, :, :], rd=[("w", "up", l)])
                ys = []
                for gu in range(2):
                    ch = gu * 22 + i
                    ps = R()
                    mm_group(ps, ps[:, 0:NT], [(wb[:, k, gu, :], xb[:, k, 0:NT]) for k in range(8)], [wb, xb])
                    h_ = hs[(2 * i + gu) % 4]
                    hv = hview(h_)
                    if prm:
                        if sbi == 0:
                            V([], [h_], lambda e, hv=hv: e.memset(hv[:, :, 0:2], 0.0))
                        else:
                            A([hl], [h_], lambda e, hv=hv, ch=ch: e.activation(out=hv[:, 0, 0:2], in_=hl[:, ch, :], func=AF.Copy))
                    else:
                        st_ = stfc[(2 * i + gu) % 4]
                        dma_in("s", st_, st_[:, :], stfT[l, ch * 128:(ch + 1) * 128, :])
                        A([st_], [h_], lambda e, hv=hv, st_=st_: e.activation(
                            out=hv[:, :, 0:2], in_=st_[:, :].rearrange("p (b i) -> p b i", i=2), func=AF.Copy))
                    A([ps], [h_], lambda e, hv=hv, ps=ps: e.activation(
                        out=hv[:, :, 2:2 + tt], in_=ps[:, 0:NT].rearrange("p (b t) -> p b t", b=nbk), func=AF.Copy))
                    if prm:
                        A([h_], [hl], lambda e, hv=hv, ch=ch: e.activation(out=hl[:, ch, :], in_=hv[:, 0, NT:NT + 2], func=AF.Copy))
                    else:
                        dma_out("s", nff_s[l, ch * 128:(ch + 1) * 128, :].rearrange("p (b i) -> p b i", i=2), h_, hv[:, :, 4:6])
                    yy = yv[(2 * i + gu) % 4]
                    yyv = yy[:, 0:NT].rearrange("p (b t) -> p b t", b=nbk)
                    w = lambda k_, ch=ch: cvf_t[:, (l * NCH_F + ch) * 3 + k_:(l * NCH_F + ch) * 3 + k_ + 1]
                    V([ps, cvf_t], [yy], lambda e, ps=ps, yyv=yyv, w=w: e.tensor_scalar(
                        out=yyv, in0=ps[:, 0:NT].rearrange("p (b t) -> p b t", b=nbk), scalar1=w(2), scalar2=None, op0=ALU.mult))
                    V([h_, yy], [yy], lambda e, hv=hv, yyv=yyv, w=w: e.scalar_tensor_tensor(
                        out=yyv, in0=hv[:, :, 1:1 + tt], scalar=w(1), in1=yyv, op0=ALU.mult, op1=ALU.add))
                    V([h_, yy], [yy], lambda e, hv=hv, yyv=yyv, w=w: e.scalar_tensor_tensor(
                        out=yyv, in0=hv[:, :, 0:tt], scalar=w(0), in1=yyv, op0=ALU.mult, op1=ALU.add))
                    ys.append(yy)
                A([ys[0]], [ys[0]], lambda e, y0=ys[0]: e.activation(out=y0[:, 0:NT], in_=y0[:, 0:NT], func=AF.Silu))
                V([ys[0], ys[1]], [sc], lambda e, i=i, y0=ys[0], y1=ys[1]: e.tensor_tensor(
                    out=actT[:, i * NT:(i + 1) * NT], in0=y0[:, 0:NT], in1=y1[:, 0:NT], op=ALU.mult))
            if prm and sbi == S // NTP - 1:
                dma_out("s", nff_p[l, :, :].rearrange("(c p) i -> p c i", p=128), hl, hl[:, :, :])
            for oc in range(8):
                ps = R()
                for hf in range(2):
                    wb = wdnb[wd_n[0] % 3]; wd_n[0] += 1
                    dma_in("s", wb, wb[:, :, :], wds[l, oc, hf, :, :, :], rd=[("w", "dn", l)])
                    for i in range(11):
                        ii = hf * 11 + i
                        T([wb, sc], [ps], lambda e, ps=ps, wb=wb, i=i, ii=ii: e.matmul(
                            ps[:, 0:NT], wb[:, i, :], actT[:, ii * NT:(ii + 1) * NT], start=(ii == 0), stop=(ii == 21)))
                V([ps, xt], [xt], lambda e, ps=ps, oc=oc: e.scalar_tensor_tensor(
                    out=xt[:, oc, 0:NT], in0=xt[:, oc, 0:NT], scalar=ALPHA, in1=ps[:, 0:NT], op0=ALU.mult, op1=ALU.add))
            layer_norm(l, 1, NT)
            if l < L - 1:
                dst = (hsc_p if prm else hsc_s)[l % 2]
                dma_out("s", dst[:, tok0:tok0 + NT].rearrange("(c p) t -> p c t", p=128), xt, xt[:, :, 0:NT],
                        wr=[("h", prm, l % 2, sbi)])
            else:
                nb_ = max(1, NT // 128)
                tw = min(NT, 128)
                for b_ in range(nb_):
                    for c in range(8):
                        ps = R()
                        T([xt, ident], [ps], lambda e, ps=ps, c=c, b_=b_: e.transpose(
                            ps[0:tw, 0:128], xt[:, c, b_ * tw:(b_ + 1) * tw], ident[:, :]))
                        A([ps], [sc], lambda e, ps=ps, c=c: e.activation(
                            out=sc[0:tw, c * 128:(c + 1) * 128], in_=ps[0:tw, 0:128], func=AF.Copy))
                    ydst = y_p if prm else y_s
                    dma_out("s", ydst[tok0 + b_ * tw:tok0 + (b_ + 1) * tw, :], sc, sc[0:tw, 0:1024])

        def layer_norm(l, which, NT):
            sq = sc[:, 0:8 * NTP].rearrange("p (c t) -> p c t", c=8)
            A([xt], [sc], lambda e: e.activation(out=sq[:, :, 0:NT], in_=xt[:, :, 0:NT], func=AF.Square))
            p1 = R(); p2 = R()
            mm_group(p1, p1[:, 0:NT], [(onesf[:, :], xt[:, c, 0:NT]) for c in range(8)], [onesf, xt])
            mm_group(p2, p2[:, 0:NT], [(onesf[:, :], sq[:, c, 0:NT]) for c in range(8)], [onesf, sc])
            A([p1], [mt], lambda e: e.mul(out=mt[:, 0:NT], in_=p1[:, 0:NT], mul=1.0 / D))
            V([mt], [msq], lambda e: e.tensor_tensor(out=msq[:, 0:NT], in0=mt[:, 0:NT], in1=mt[:, 0:NT], op=ALU.mult))
            V([p2, msq], [rt2], lambda e: e.scalar_tensor_tensor(
                out=rt2[:, 0:NT], in0=p2[:, 0:NT], scalar=1.0 / D, in1=msq[:, 0:NT], op0=ALU.mult, op1=ALU.subtract))
            V([rt2], [rt2], lambda e: e.tensor_scalar(out=rt2[:, 0:NT], in0=rt2[:, 0:NT], scalar1=LN_EPS, scalar2=None, op0=ALU.add))
            A([rt2], [rt2], lambda e: e.activation(out=rt2[:, 0:NT], in_=rt2[:, 0:NT], func=AF.Sqrt))
            V([rt2], [rt2], lambda e: e.reciprocal(out=rt2[:, 0:NT], in_=rt2[:, 0:NT]))
            V([xt, mt], [xt], lambda e: e.tensor_tensor(
                out=xt[:, :, 0:NT], in0=xt[:, :, 0:NT], in1=mt[:, 0:NT].unsqueeze(1).to_broadcast([128, 8, NT]), op=ALU.subtract))
            V([xt, rt2], [xt], lambda e: e.tensor_tensor(
                out=xt[:, :, 0:NT], in0=xt[:, :, 0:NT], in1=rt2[:, 0:NT].unsqueeze(1).to_broadcast([128, 8, NT]), op=ALU.mult))
            for c in range(8):
                gi = l * 32 + (2 * which) * 8 + c
                bi = l * 32 + (2 * which + 1) * 8 + c
                eng = V
                eng([xt, lng_t], [xt], lambda e, c=c, gi=gi, bi=bi: e.tensor_scalar(
                    out=xt[:, c, 0:NT], in0=xt[:, c, 0:NT], scalar1=lng_t[:, gi:gi + 1], scalar2=lng_t[:, bi:bi + 1],
                    op0=ALU.mult, op1=ALU.add))

        def rotary(tile_t, ap3, H, nq):
            x1 = ap3[:, :, 0:8]; x2 = ap3[:, :, 8:16]
            cb = cs[0:nq, 0:8].unsqueeze(1).to_broadcast([nq, H, 8])
            sb_ = cs[0:nq, 8:16].unsqueeze(1).to_broadcast([nq, H, 8])
            t = [rt[0:nq, i, 0:H, :] for i in range(4)]
            V([tile_t, cs], [rt], lambda e: e.tensor_tensor(out=t[0], in0=x1, in1=cb, op=ALU.mult))
            V([tile_t, cs], [rt], lambda e: e.tensor_tensor(out=t[1], in0=x2, in1=sb_, op=ALU.mult))
            V([tile_t, cs], [rt], lambda e: e.tensor_tensor(out=t[2], in0=x2, in1=cb, op=ALU.mult))
            V([tile_t, cs], [rt], lambda e: e.tensor_tensor(out=t[3], in0=x1, in1=sb_, op=ALU.mult))
            V([rt], [tile_t], lambda e: e.tensor_tensor(out=x1, in0=t[0], in1=t[1], op=ALU.subtract))
            V([rt], [tile_t], lambda e: e.tensor_tensor(out=x2, in0=t[2], in1=t[3], op=ALU.add))

        pg_n = [0]

        def stage_pages(l, b):
            dma_in("s", ptb, ptb[:, :], pt[0:1, b * NPG:(b + 1) * NPG].to_broadcast([128, NPG]))
            V([ptb, iol], [idx], lambda e: e.tensor_scalar(out=idx[:, :], in0=ptb[:, :], scalar1=128.0,
                                                           scalar2=iol[:, l:l + 1], op0=ALU.mult, op1=ALU.add))
            for j in range(NPG):
                s_ = pg_n[0] % 2; pg_n[0] += 1
                ia = idx[:, j:j + 1]
                pg = pgs[s_]
                pbk = pbank[6 + (j % 2)]
                DG([idx], [buf(pg)], lambda e, pg=pg, ia=ia: e.indirect_dma_start(
                    out=pg[:, :], out_offset=None, in_=ckv[:, :],
                    in_offset=bass.IndirectOffsetOnAxis(ap=ia, axis=0)))
                V([pg], [kpb[s_]], lambda e, s_=s_, pg=pg: e.tensor_copy(out=kpb[s_][:, :], in_=pg[:, 0:128]))
                V([pg], [ipb[s_]], lambda e, s_=s_, pg=pg: e.tensor_copy(
                    out=ipb[s_][:, :].rearrange("p (a d) -> p a d", a=2), in_=pg[:, 256:320].unsqueeze(1).to_broadcast([128, 2, 64])))
                A([pg], [Vp], lambda e, pg=pg, j=j: e.activation(
                    out=Vp[:, j, :, 0:64], in_=pg[:, 128:256].rearrange("p (g d) -> p g d", g=2), func=AF.Copy))
                T([kpb[s_], identb], [pbk], lambda e, s_=s_, pbk=pbk: e.transpose(pbk[:, 0, :], kpb[s_][:, :], identb[:, :]))
                T([ipb[s_], identb], [pbk], lambda e, s_=s_, pbk=pbk: e.transpose(pbk[:, 1, :], ipb[s_][:, :], identb[:, :]))
                A([pbk], [kT], lambda e, j=j, pbk=pbk: e.activation(out=kT[:, j * 128:(j + 1) * 128], in_=pbk[:, 0, :], func=AF.Copy))
                A([pbk], [kiT], lambda e, j=j, pbk=pbk: e.activation(out=kiT[:, j * 128:(j + 1) * 128], in_=pbk[:, 1, :], func=AF.Copy))

        def tm_block(l, prm, NT, nq, t0, g0, koff, vchunk, nkeys):
            Prog.MARKS.append(('tm', l, prm, g0, P.nops))
            if prm:
                dma_in("s", cs, cs[0:nq, :], csp[g0:g0 + nq, :])
            else:
                dma_in("s", cs, cs[0:nq, :], css[0:nq, :])
            xcols = lambda k: xb[:, k, t0:t0 + nq]
            groups = [(1024, 512), (1536, 512), (2048, 392), (2440, 256)]
            pss = []
            pj = 4
            for gi, (c0, cw) in enumerate(groups):
                ps = R()
                for p0 in range(0, cw, 256):
                    pw = min(256, cw - p0)
                    wb = winb[wi_n[0] % 3]; wi_n[0] += 1
                    dma_in("s", wb, wb[:, :, 0:pw], wis[l, pj, :, :, 0:pw], rd=[("w", "in", l)])
                    pj += 1
                    mm_group(ps, ps[0:nq, p0:p0 + pw], [(xcols(k), wb[:, k, 0:pw]) for k in range(8)], [wb, xb])
                if gi == 0:
                    A([ps], [qq], lambda e, ps=ps: e.activation(out=qq[0:nq, 0:512], in_=ps[0:nq, 0:512], func=AF.Copy))
                elif gi == 1:
                    A([ps], [qq], lambda e, ps=ps: e.activation(out=qq[0:nq, 512:1024], in_=ps[0:nq, 0:512], func=AF.Copy))
                elif gi == 2:
                    A([ps], [kv], lambda e, ps=ps: e.activation(out=kv[0:nq, :], in_=ps[0:nq, 0:392], func=AF.Copy))
                else:
                    psvc = ps
            V([psvc], [bst], lambda e: e.bn_stats(out=bst[0:nq, 0:6], in_=psvc[0:nq, 0:256]))
            V([bst], [sm], lambda e: e.bn_aggr(out=sm[0:nq, 0:2], in_=bst[0:nq, 0:6]))
            V([sm], [sm], lambda e: e.tensor_scalar(out=sm[0:nq, 2:3], in0=sm[0:nq, 1:2], scalar1=LN_EPS, scalar2=None, op0=ALU.add))
            A([sm], [sm], lambda e: e.activation(out=sm[0:nq, 3:4], in_=sm[0:nq, 2:3], func=AF.Sqrt))
            V([sm], [sm], lambda e: e.reciprocal(out=sm[0:nq, 4:5], in_=sm[0:nq, 3:4]))
            V([psvc, sm], [vn], lambda e: e.tensor_scalar(out=vn[0:nq, :], in0=psvc[0:nq, 0:256], scalar1=sm[0:nq, 0:1],
                                                          scalar2=sm[0:nq, 4:5], op0=ALU.subtract, op1=ALU.mult))
            V([vn, lncv_t], [vn], lambda e: e.tensor_tensor(out=vn[0:nq, :], in0=vn[0:nq, :],
                                                            in1=lncv_t[0:nq, 0:256], op=ALU.mult))
            V([vn, lncv_t], [vn], lambda e: e.tensor_tensor(out=vn[0:nq, :], in0=vn[0:nq, :],
                                                            in1=lncv_t[0:nq, 256:512], op=ALU.add))
            A([vn], [vnb], lambda e: e.activation(out=vnb[0:nq, :], in_=vn[0:nq, :], func=AF.Copy))
            if not prm:
                dma_out("s", nch_s[l, g0:g0 + nq, :], vn, vn[0:nq, :])
            rotary(qq, qq[0:nq, :].rearrange("p (h d) -> p h d", d=64), 16, nq)
            rotary(kv, kv[0:nq, 0:128].rearrange("p (h d) -> p h d", d=64), 2, nq)
            rotary(kv, kv[0:nq, 256:384].rearrange("p (h d) -> p h d", d=64), 2, nq)
            ok, ov, oki = (nk_p, nv_p, nki_p) if prm else (nk_s, nv_s, nki_s)
            dma_out("s", ok[l, g0:g0 + nq, :], kv, kv[0:nq, 0:128])
            dma_out("s", ov[l, g0:g0 + nq, :], kv, kv[0:nq, 128:256])
            dma_out("s", oki[l, g0:g0 + nq, :], kv, kv[0:nq, 256:320])
            V([kv], [sm], lambda e: e.tensor_scalar(out=sm[0:nq, 8:16], in0=kv[0:nq, 384:392], scalar1=0.0, scalar2=2.0,
                                                    op0=ALU.is_gt, op1=ALU.mult))
            V([sm], [sm], lambda e: e.tensor_scalar(out=sm[0:nq, 8:16], in0=sm[0:nq, 8:16], scalar1=-1.0, scalar2=None, op0=ALU.add))
            V([kv, sm], [sm], lambda e: e.scalar_tensor_tensor(out=sm[0:nq, 16:24], in0=kv[0:nq, 384:392],
                                                               scalar=float(8.0 ** -0.5 * 0.125), in1=sm[0:nq, 8:16],
                                                               op0=ALU.mult, op1=ALU.mult))
            V([qq, sm], [qqb], lambda e: e.tensor_tensor(
                out=qqb[0:nq, 512:1024].rearrange("p (h d) -> p h d", d=64),
                in0=qq[0:nq, 512:1024].rearrange("p (h d) -> p h d", d=64),
                in1=sm[0:nq, 16:24].unsqueeze(2).to_broadcast([nq, 8, 64]), op=ALU.mult))
            A([qq], [qqb], lambda e: e.activation(out=qqb[0:nq, 0:512], in_=qq[0:nq, 0:512], func=AF.Copy))
            A([kv], [kvb], lambda e: e.activation(out=kvb[0:nq, :], in_=kv[0:nq, :], func=AF.Copy))
            V([kv], [Vp], lambda e: e.tensor_copy(out=Vp[0:nq, vchunk, :, 0:64],
                                                  in_=kv[0:nq, 128:256].rearrange("p (g d) -> p g d", g=2)))
            for h in range(8):
                V([ident, sm], [Dm], lambda e, h=h: e.tensor_scalar(
                    out=Dm[0:nq, h, 0:nq], in0=ident[0:nq, 0:nq], scalar1=sm[0:nq, 8 + h:9 + h], scalar2=None, op0=ALU.mult))
            for r in range(8):
                T([qqb, identb], [pbank[7]], lambda e, r=r: e.transpose(
                    ptb_bf[:, r, 0:nq], qqb[0:nq, r * 128:(r + 1) * 128], identb[0:nq, 0:nq]))
            A([pbank[7]], [qT], lambda e: e.activation(out=qT[:, :, 0:nq], in_=ptb_bf[:, 0:4, 0:nq], func=AF.Copy))
            A([pbank[7]], [qiT], lambda e: e.activation(out=qiT[:, :, 0:nq], in_=ptb_bf[:, 4:8, 0:nq], func=AF.Copy))
            T([kvb, identb], [pbank[6]], lambda e: e.transpose(pt2_bf[:, 0, 0:nq], kvb[0:nq, 0:128], identb[0:nq, 0:nq]))
            T([kvb, identb], [pbank[6]], lambda e: e.transpose(pt2_bf[:, 1, 0:nq], kvb[0:nq, 256:384], identb[0:nq, 0:nq]))
            A([pbank[6]], [kT], lambda e: e.activation(out=kT[:, koff:koff + nq], in_=pt2_bf[:, 0, 0:nq], func=AF.Copy))
            A([pbank[6]], [kiT], lambda e: e.activation(out=kiT[:, koff:koff + nq], in_=pt2_bf[:, 1, 0:nq], func=AF.Copy))
            cps = R()
            for c in range(2):
                for hh in range(2):
                    h = 2 * c + hh
                    T([vnb, wmT], [cps], lambda e, c=c, hh=hh, h=h: e.matmul(
                        cps[64 * hh:64 * hh + 64, c * nq:(c + 1) * nq], vnb[0:nq, h * 64:(h + 1) * 64], wmT[0:nq, h, 0:nq],
                        start=True, stop=True))
            for c in range(2):
                V([cps, bspc_t], [ctmp], lambda e, c=c: e.tensor_tensor(
                    out=ctmp[:, 0:nq], in0=cps[:, c * nq:(c + 1) * nq],
                    in1=bspc_t[:, c * 128:c * 128 + nq], op=ALU.add))
                V([ctmp, fm], [cT], lambda e, c=c: e.tensor_tensor(
                    out=cT[:, c, t0:t0 + nq], in0=ctmp[:, 0:nq], in1=fm[:, 6 + c, t0:t0 + nq], op=ALU.mult))
            chunks = [(c0, min(512, nkeys - c0)) for c0 in range(0, nkeys, 512)]
            steps = [(ci, h) for ci in range(len(chunks)) for h in range(8)]
            pend = []

            def fin(i):
                ci, h = steps[i]
                c0, w = chunks[ci]
                ps, rr = pend[i]
                acc = pbank[4 + ci % 2]
                A([ps], [rr], lambda e: e.activation(out=rr[0:nq, 0:w], in_=ps[0:nq, 0:w], func=AF.Relu))
                T([Dm, rr], [acc], lambda e: e.matmul(acc[0:nq, 0:w], Dm[0:nq, h, 0:nq], rr[0:nq, 0:w],
                                                     start=(h == 0), stop=(h == 7)))
                if h == 7:
                    A([acc], [sc], lambda e: e.activation(out=sc[0:nq, c0:c0 + w], in_=acc[0:nq, 0:w], func=AF.Copy))
            for i, (ci, h) in enumerate(steps):
                c0, w = chunks[ci]
                ps = R(); rr = rh[i % 4]
                pend.append((ps, rr))
                hp = 64 * (h % 2)
                T([qiT, kiT], [ps], lambda e, ps=ps, hp=hp, h=h, c0=c0, w=w: e.matmul(
                    ps[0:nq, 0:w], qiT[hp:hp + 64, h // 2, 0:nq], kiT[hp:hp + 64, c0:c0 + w], start=True, stop=True))
                if i >= 2:
                    fin(i - 2)
            for i in range(max(0, len(steps) - 2), len(steps)):
                fin(i)
            ksel = min(256, (S if prm else PAST + 4) // 4)
            V([sc], [sm], lambda e: e.tensor_reduce(out=sm[0:nq, 24:25], in_=sc[0:nq, 0:nkeys], axis=mybir.AxisListType.X, op=ALU.max))
            V([sc], [sm], lambda e: e.tensor_reduce(out=sm[0:nq, 25:26], in_=sc[0:nq, 0:nkeys], axis=mybir.AxisListType.X, op=ALU.min))
            V([sc, tneg], [sc], lambda e: e.tensor_tensor(out=sc[0:nq, koff:koff + nq], in0=sc[0:nq, koff:koff + nq],
                                                          in1=tneg[0:nq, 0:nq], op=ALU.add))
            V([sm], [sm], lambda e: e.scalar_tensor_tensor(out=sm[0:nq, 26:27], in0=sm[0:nq, 24:25], scalar=1.0,
                                                           in1=sm[0:nq, 25:26], op0=ALU.add, op1=ALU.subtract))
            lo = sm[0:nq, 25:26]; w0 = sm[0:nq, 26:27]; mid = sm[0:nq, 27:28]; cnt = sm[0:nq, 28:29]; tmp = sm[0:nq, 29:30]
            for it in range(NIT):
                cc = 2.0 ** -(it + 1)
                V([sm], [sm], lambda e, cc=cc: e.scalar_tensor_tensor(out=mid, in0=w0, scalar=cc, in1=lo, op0=ALU.mult, op1=ALU.add))
                for s0 in range(0, nkeys, 2048):
                    sw = min(2048, nkeys - s0)
                    V([sc, sm], [junk, sm], lambda e, s0=s0, sw=sw: e.tensor_scalar(
                        out=junk[0:nq, 0:sw], in0=sc[0:nq, s0:s0 + sw], scalar1=mid, scalar2=(None if s0 == 0 else cnt),
                        op0=ALU.is_ge, op1=ALU.add, accum_out=cnt))
                V([sm], [sm], lambda e: e.tensor_scalar(out=tmp, in0=cnt, scalar1=ksel - 0.5, scalar2=w0, op0=ALU.is_ge, op1=ALU.mult))
                V([sm], [sm], lambda e, cc=cc: e.scalar_tensor_tensor(out=lo, in0=tmp, scalar=cc, in1=lo, op0=ALU.mult, op1=ALU.add))
            if prm:
                kcs = [(kc, kc * 128, 128) for kc in range(koff // 128 + 1)]
            else:
                kcs = [(kc, kc * 128, 128) for kc in range(NPG)] + [(NPG, PAST, 4)]
            og = [pbank[4], pbank[5]]
            N4 = 4 * nq
            asteps = [(a, g) for a in range(len(kcs)) for g in range(2)]
            apend = []

            def afin(i):
                a, g = asteps[i]
                kc, k0, kw = kcs[a]
                st, pt_ = apend[i]
                A([st], [pt_], lambda e: e.activation(out=pt_[0:kw, 0:N4], in_=st[0:kw, 0:N4], func=AF.Exp, scale=0.125))
                T([Vp, pt_], [og[g]], lambda e: e.matmul(og[g][0:65, 0:N4], Vp[0:kw, kc, g, 0:65], pt_[0:kw, 0:N4],
                                                        start=(a == 0), stop=(a == len(kcs) - 1)))
            for i, (a, g) in enumerate(asteps):
                kc, k0, kw = kcs[a]
                if g == 0:
                    nm_ = nmc[a % 4]
                    V([sc, sm], [nm_], lambda e, nm_=nm_, k0=k0, kw=kw: e.tensor_scalar(
                        out=nm_[0:nq, 0:kw], in0=sc[0:nq, k0:k0 + kw], scalar1=lo, scalar2=NEG, op0=ALU.is_lt, op1=ALU.mult))
                st = R(); pt_ = pT[i % 4]
                apend.append((st, pt_))
                st3 = st[0:kw, 0:N4].rearrange("p (r t) -> p r t", r=4)
                T([kT, qT], [st], lambda e, st3=st3, g=g, k0=k0, kw=kw: e.matmul(
                    st3, kT[64 * g:64 * g + 64, k0:k0 + kw], qT[64 * g:64 * g + 64, :, 0:nq], start=True, stop=False))
                T([nm_, identrep], [st], lambda e, st3=st3, nm_=nm_, kw=kw: e.matmul(
                    st3, nm_[0:nq, 0:kw], identrep[0:nq, :, 0:nq], start=False, stop=True))
                if i >= 2:
                    afin(i - 2)
            for i in range(max(0, len(asteps) - 2), len(asteps)):
                afin(i)
            for g in range(2):
                A([og[g]], [srow], lambda e, g=g: e.activation(out=srow[64:65, 0:N4], in_=og[g][64:65, 0:N4], func=AF.Copy))
                pbs = R()
                T([onesf, srow], [pbs], lambda e, pbs=pbs: e.matmul(pbs[0:64, 0:N4], onesf[64:65, 0:64], srow[64:65, 0:N4],
                                                                   start=True, stop=True))
                V([pbs], [rs], lambda e, pbs=pbs: e.reciprocal(out=rs[0:64, 0:N4], in_=pbs[0:64, 0:N4]))
                V([og[g], rs], [oT], lambda e, g=g: e.tensor_tensor(
                    out=oT[0:64, g, :, t0:t0 + nq], in0=og[g][0:64, 0:N4].rearrange("p (r t) -> p r t", r=4),
                    in1=rs[0:64, 0:N4].rearrange("p (r t) -> p r t", r=4), op=ALU.mult))

        stage = sc[:, 0:2816]
        cvb = sc[:, 2816:2816 + 1408].bitcast(BF16)
        cv_n = [0]

        def conv_piece(src_ap, w, outs, key):
            DS([], [sc], lambda e: e.dma_start(out=stage[:, 0:w], in_=src_ap))
            k_ = cv_n[0] % 3; cv_n[0] += 1
            if k_ == 0:
                A([sc], [sc], lambda e: e.activation(out=cvb[:, 0:w], in_=stage[:, 0:w], func=AF.Copy))
            elif k_ == 1:
                V([sc], [sc], lambda e: e.tensor_copy(out=cvb[:, 0:w], in_=stage[:, 0:w]))
            else:
                G([sc], [sc], lambda e: e.tensor_copy(out=cvb[:, 0:w], in_=stage[:, 0:w]))
            for (dst, src) in outs:
                DS([sc], [key], lambda e, dst=dst, src=src: e.dma_start(out=dst, in_=src))
        for l in range(L):
            for k in range(8):
                conv_piece(w_in[l, k * 128:(k + 1) * 128, :], WCOLS, [
                    (wis[l, 0:9, :, k, :].rearrange("j p c -> p j c"), cvb[:, 0:2304].rearrange("p (j c) -> p j c", c=256)),
                    (wis[l, 9, :, k, 0:136], cvb[:, 2304:2440]),
                    (wis[l, 10, :, k, :], cvb[:, 2440:2696])], ("w", "in", l))
            for r in range(8):
                if r < 2 or r >= 6:
                    kc = r if r < 2 else 10 + (r - 6)
                    outs = [(wos[l, :, :, kc, :].rearrange("o p c -> p o c"), cvb[:, 0:1024].rearrange("p (o c) -> p o c", c=128))]
                else:
                    hA = 2 * (r - 2)
                    outs = [(wos[l, :, 0:64, 2 + hA, :].rearrange("o p c -> p o c"), cvb[0:64, 0:1024].rearrange("p (o c) -> p o c", c=128)),
                            (wos[l, :, 0:64, 3 + hA, :].rearrange("o p c -> p o c"), cvb[64:128, 0:1024].rearrange("p (o c) -> p o c", c=128))]
                conv_piece(w_o[l, r * 128:(r + 1) * 128, :], 1024, outs, ("w", "o", l))
            for k in range(8):
                for gu in range(2):
                    conv_piece(w_up[l, k * 128:(k + 1) * 128, gu * DFF:(gu + 1) * DFF], DFF, [
                        (wus[l, :, :, k, gu, :].rearrange("i p c -> p i c"), cvb[:, 0:DFF].rearrange("p (i c) -> p i c", c=128))],
                        ("w", "up", l))
            for i in range(22):
                conv_piece(w_dn[l, i * 128:(i + 1) * 128, :], 1024, [
                    (wds[l, :, i // 11, :, i % 11, :].rearrange("o p c -> p o c"), cvb[:, 0:1024].rearrange("p (o c) -> p o c", c=128))],
                    ("w", "dn", l))

        for l in range(L):
            dma_in("s", lncv_t, lncv_t[:, :], lncv[:, l * 512:(l + 1) * 512])
            dma_in("s", bspc_t, bspc_t[:, :], bspc[:, l * 256:(l + 1) * 256])
            dma_in("s", wtmp, wtmp[:, :], wspT[:, l * 512:l * 512 + 128])
            for h in range(4):
                if h > 0:
                    dma_in("s", wtmp, wtmp[:, :], wspT[:, (l * 4 + h) * 128:(l * 4 + h + 1) * 128])
                V([wtmp, triu], [wmT], lambda e, h=h: e.tensor_tensor(out=wmT[:, h, :], in0=wtmp[:, :], in1=triu[:, :], op=ALU.mult))
            for sbi in range(S // NTP):
                process_sb(l, "p", sbi)
            process_sb(l, "s", 0)
        P.finish()

        with nc.Block() as block:
            @block.tensor
            def _(e):
                for f in P.q["tensor"]:
                    f(e)

            @block.vector
            def _(e):
                for f in P.q["vector"]:
                    f(e)

            @block.scalar
            def _(e):
                for f in P.q["scalar"]:
                    f(e)

            @block.gpsimd
            def _(e):
                for f in P.q["gpsimd"]:
                    f(e)

            @block.sync
            def _(e):
                for f in P.q["sync"]:
                    f(e)
    return nc, P.n


def _host_layout(inputs, cfg, core):
    S, L, NB, NPG, NPOOL = cfg["S"], cfg["L"], cfg["NB"], cfg["NPG"], cfg["NPOOL"]
    f = lambda a: np.ascontiguousarray(a, dtype=np.float32)
    w_in = np.asarray(inputs["w_in"])
    qperm = np.concatenate([768 + h * 64 + np.arange(64) for h in (0, 4, 1, 5, 2, 6, 3, 7)])
    r = np.arange
    cols = np.concatenate([r(0, 256), r(256, 512), r(512, 768), r(2120, 2376), qperm, r(1536, 2048),
                           r(1280, 1408), r(1408, 1536), r(2048, 2112), r(2048, 2112), r(2112, 2120), r(2376, 2632)])
    assert cols.size == WCOLS
    m = {}
    m["xT"] = f(np.asarray(inputs["x_prompt"])[core].T)
    xs = np.asarray(inputs["x_sample"])[core * NB:(core + 1) * NB].reshape(NB * 4, D)
    m["xsT"] = f(xs.T)
    m["w_in"] = f(w_in[:, :, cols])
    m["w_o"] = f(inputs["w_o"]); m["w_up"] = f(inputs["w_up"]); m["w_dn"] = f(inputs["w_down"])
    ca = np.asarray(inputs["conv_a"])
    m["cva"] = f(ca.reshape(L, 3, 2, 128).transpose(3, 0, 2, 1).reshape(128, L * 6))
    cf = np.asarray(inputs["conv_f"])
    m["cvf"] = f(cf.reshape(L, 3, NCH_F, 128).transpose(3, 0, 2, 1).reshape(128, L * NCH_F * 3))
    ln = np.stack([np.asarray(inputs[k]) for k in ("ln1_g", "ln1_b", "ln2_g", "ln2_b")], axis=1)
    m["lng"] = f(ln.reshape(L, 4, 8, 128).transpose(3, 0, 1, 2).reshape(128, L * 32))
    lcv = np.stack([np.asarray(inputs["ln_cv_g"]), np.asarray(inputs["ln_cv_b"])], axis=1)
    m["lncv"] = f(np.broadcast_to(lcv.reshape(1, L * 512), (128, L * 512)))
    bs = np.asarray(inputs["b_sp"])
    bc = bs.reshape(L, 2, 2, 128)
    m["bspc"] = f(np.repeat(bc.transpose(2, 0, 1, 3), 64, axis=0).reshape(128, L * 256))
    ws = np.asarray(inputs["w_sp"])
    m["wspT"] = f(ws.transpose(3, 0, 1, 2).reshape(128, L * 512))
    half = 8
    inv = (np.float32(500000.0) ** (-np.arange(half, dtype=np.float32) / np.float32(half))).astype(np.float32)
    def cstab(pos):
        ang = pos.astype(np.float32)[:, None] * inv[None, :]
        return np.concatenate([np.cos(ang), np.sin(ang)], axis=1).astype(np.float32)
    m["csp"] = cstab(np.arange(S))
    m["css"] = cstab(NPG * 128 + np.arange(4))
    m["cident"] = np.eye(128, dtype=np.float32)
    ii = np.arange(128)
    m["ctriu"] = (ii[:, None] <= ii[None, :]).astype(np.float32)
    m["ctneg"] = np.where(ii[None, :] <= ii[:, None], 0.0, -1e30).astype(np.float32)
    m["ciota"] = ii.astype(np.float32).reshape(128, 1)
    m["ckv"] = np.concatenate([np.asarray(inputs["cache_k"], dtype=np.float32).reshape(L * NPOOL * 128, 128),
                               np.asarray(inputs["cache_v"], dtype=np.float32).reshape(L * NPOOL * 128, 128),
                               np.asarray(inputs["cache_kidx"], dtype=np.float32).reshape(L * NPOOL * 128, 64)], axis=1)
    m["pt"] = np.ascontiguousarray(np.asarray(inputs["page_table"])[core * NB:(core + 1) * NB].reshape(1, NB * NPG), dtype=np.int32)
    sc_ = np.asarray(inputs["state_conv"])[:, core * NB:(core + 1) * NB]
    m["stcT"] = f(sc_.transpose(0, 3, 1, 2).reshape(L, 256, NB * 2))
    sf_ = np.asarray(inputs["state_ffn_conv"])[:, core * NB:(core + 1) * NB]
    m["stfT"] = f(sf_.transpose(0, 3, 1, 2).reshape(L, 2 * DFF, NB * 2))
    return m


_CACHE = {}


def run(inputs, cfg, ncores=2):
    key = tuple(sorted(cfg.items()))
    if key not in _CACHE:
        _CACHE[key] = build(cfg)[0]
    nc = _CACHE[key]
    maps = [_host_layout(inputs, cfg, c) for c in range(ncores)]
    res = run_bass_kernel_spmd(nc, maps, core_ids=list(range(ncores))).results
    S, L, NB = cfg["S"], cfg["L"], cfg["NB"]
    cat = lambda k, ax: np.concatenate([np.asarray(r[k]) for r in res], axis=ax)
    st = lambda k: np.stack([np.asarray(r[k]) for r in res], axis=0)
    y_p = st("y_p")
    y_s = cat("y_s", 0).reshape(ncores * NB, 4, D)
    nk_p = st("nk_p").transpose(1, 0, 2, 3).reshape(L, ncores, S, 2, 64)
    nv_p = st("nv_p").transpose(1, 0, 2, 3).reshape(L, ncores, S, 2, 64)
    nki_p = st("nki_p").transpose(1, 0, 2, 3)
    nk_s = cat("nk_s", 1).reshape(L, ncores * NB, 4, 2, 64)
    nv_s = cat("nv_s", 1).reshape(L, ncores * NB, 4, 2, 64)
    nki_s = cat("nki_s", 1).reshape(L, ncores * NB, 4, 64)
    ncv_p = st("ncv_p").transpose(1, 0, 3, 2)
    ncv_s = cat("ncv_s", 2).reshape(L, 256, ncores * NB, 2).transpose(0, 2, 3, 1)
    nff_p = st("nff_p").transpose(1, 0, 3, 2)
    nff_s = cat("nff_s", 2).reshape(L, 2 * DFF, ncores * NB, 2).transpose(0, 2, 3, 1)
    nch_s = cat("nch_s", 1).reshape(L, ncores * NB, 4, 256)
    outs = (y_p, y_s, nk_p, nk_s, nv_p, nv_s, nki_p, nki_s, ncv_p, ncv_s, nff_p, nff_s, nch_s)
    return tuple(np.ascontiguousarray(o, dtype=np.float32) for o in outs)


def kernel(**inputs):
    xp = np.asarray(inputs["x_prompt"])
    ptab = np.asarray(inputs["page_table"])
    cfg = dict(S=xp.shape[1], L=np.asarray(inputs["w_in"]).shape[0], NB=ptab.shape[0] // 2,
               NPG=ptab.shape[1], NPOOL=np.asarray(inputs["cache_k"]).shape[1])
    return run(inputs, cfg, ncores=2)
```
